# Optimizing a Trainium2 kernel written in Bass

```python
import jax, jax.numpy as jnp
from jax import lax
import numpy as np

D_MODEL = 1024
BATCH = 8
SEQ = 4096
DEPTH = 4

HGRN_KDIM = 128
HGRN_VDIM = 128
HGRN_HEADS = D_MODEL // HGRN_KDIM
HGRN_WIDTH = HGRN_HEADS * HGRN_KDIM
GLA_CHUNK = 64
SG_CHUNK = 128
SG_GROUP_CH = 128
SG_GROUPS = D_MODEL // SG_GROUP_CH
SG_WIDTH = SG_GROUPS * SG_GROUP_CH
IN_SPLITS = (HGRN_WIDTH, HGRN_WIDTH, HGRN_WIDTH, HGRN_WIDTH, HGRN_WIDTH,
             SG_WIDTH, SG_WIDTH, SG_WIDTH, D_MODEL, D_MODEL)
IN_WIDTH = 5 * HGRN_WIDTH + 3 * SG_WIDTH + 2 * D_MODEL
RMS_EPS = 1e-6
LN_EPS = 1e-5
LB_FLOOR = 1e-20

kernel_name = "hgrn2_spatial_gating_hybrid_encoder"


def _rmsnorm(x, w, eps=RMS_EPS):
    xf = x.astype(jnp.float32)
    y = xf * lax.rsqrt(jnp.mean(xf * xf, axis=-1, keepdims=True) + eps)
    return (y * w.astype(jnp.float32)).astype(x.dtype)


def _layernorm(x, w, b, eps=LN_EPS):
    xf = x.astype(jnp.float32)
    mu = jnp.mean(xf, axis=-1, keepdims=True)
    var = jnp.mean(jnp.square(xf - mu), axis=-1, keepdims=True)
    y = (xf - mu) * lax.rsqrt(var + eps)
    return (y * w.astype(jnp.float32) + b.astype(jnp.float32)).astype(x.dtype)


def _gla_chunked(q, k, v, log_f):
    b, h, s, dk = q.shape
    dv = v.shape[-1]
    n = s // GLA_CHUNK

    def to_chunks(t):
        return jnp.moveaxis(t.reshape(b, h, n, GLA_CHUNK, t.shape[-1]), 2, 0)

    qc, kc, vc = to_chunks(q), to_chunks(k), to_chunks(v)
    gc = jnp.cumsum(to_chunks(log_f), axis=-2)
    mask = jnp.tril(jnp.ones((GLA_CHUNK, GLA_CHUNK), dtype=bool))[:, :, None]

    def step(state, xs):
        qb, kb, vb, gb = xs
        diff = gb[..., :, None, :] - gb[..., None, :, :]
        decay = jnp.where(mask, jnp.exp(jnp.where(mask, diff, 0.0)), 0.0)
        attn = jnp.einsum('bhtd,bhsd,bhtsd->bhts', qb, kb, decay)
        o = jnp.einsum('bhts,bhsv->bhtv', attn, vb)
        o = o + jnp.einsum('bhtd,bhdv->bhtv', qb * jnp.exp(gb), state)
        g_last = gb[..., -1:, :]
        k_dec = kb * jnp.exp(g_last - gb)
        state = state * jnp.exp(g_last[..., 0, :])[..., None] + jnp.einsum('bhsd,bhsv->bhdv', k_dec, vb)
        return state, o

    state0 = jnp.zeros((b, h, dk, dv), jnp.float32)
    _, outs = lax.scan(step, state0, (qc, kc, vc, gc))
    return jnp.moveaxis(outs, 0, 2).reshape(b, h, s, dv)


def _hgrn2_forget(f_raw, lb):
    z = f_raw.astype(jnp.float32)
    k = (1.0 - lb) * jax.nn.sigmoid(-z)
    log_f = jnp.logaddexp(jnp.log(jnp.maximum(lb, LB_FLOOR)),
                          jnp.log1p(-lb) + jax.nn.log_sigmoid(z))
    return k, log_f


def _hgrn2_branch(q_raw, ffwd_raw, fbwd_raw, i_raw, g_raw, lb_fwd, lb_bwd, gnorm_w):
    b, s, _ = q_raw.shape

    def heads(t):
        return t.reshape(b, s, HGRN_HEADS, -1).transpose(0, 2, 1, 3).astype(jnp.float32)

    def flip(t):
        return jnp.flip(t, axis=2)

    q = heads(jax.nn.silu(q_raw)) * (HGRN_KDIM ** -0.5)
    v = heads(i_raw)
    k_f, g_f = _hgrn2_forget(ffwd_raw, lb_fwd)
    k_b, g_b = _hgrn2_forget(fbwd_raw, lb_bwd)
    o_fwd = _gla_chunked(q, heads(k_f), v, heads(g_f))
    o_bwd = flip(_gla_chunked(flip(q), flip(heads(k_b)), flip(v), flip(heads(g_b))))
    o = (o_fwd + o_bwd).transpose(0, 2, 1, 3)
    o = _rmsnorm(o, gnorm_w).reshape(b, s, HGRN_HEADS * HGRN_VDIM)
    return (o * jax.nn.silu(g_raw.astype(jnp.float32))).astype(q_raw.dtype)


def _spatial_gating_branch(u_raw, v_raw, g_raw, ln_w, ln_b, w_s, b_s):
    b, s, _ = u_raw.shape
    u = jax.nn.gelu(u_raw)
    v = _layernorm(jax.nn.gelu(v_raw), ln_w, ln_b)
    vc = v.reshape(b, s // SG_CHUNK, SG_CHUNK, SG_GROUPS, SG_GROUP_CH)
    mixed = jnp.einsum('gts,bnsgc->bntgc', w_s, vc) + b_s.T[None, None, :, :, None]
    return u * mixed.reshape(b, s, SG_WIDTH) * jax.nn.silu(g_raw)


def setup_inputs(seed: int = 0) -> dict:
    key = jax.random.key(seed)
    ks = jax.random.split(key, 16)
    f32 = jnp.float32
    x = jax.random.normal(ks[0], (BATCH, SEQ, D_MODEL), f32)
    norm_w = 1.0 + 0.05 * jax.random.normal(ks[1], (DEPTH, D_MODEL), f32)
    w_in = jax.random.normal(ks[2], (DEPTH, D_MODEL, IN_WIDTH), f32) * D_MODEL ** -0.5
    lower_bounds = 0.1 * jax.random.normal(ks[3], (DEPTH, 2, HGRN_WIDTH), f32)
    gnorm_w = 1.0 + 0.05 * jax.random.normal(ks[4], (DEPTH, HGRN_VDIM), f32)
    ln_w = 1.0 + 0.05 * jax.random.normal(ks[5], (DEPTH, SG_WIDTH), f32)
    ln_b = 0.02 * jax.random.normal(ks[6], (DEPTH, SG_WIDTH), f32)
    w_s = jax.random.normal(ks[7], (DEPTH, SG_GROUPS, SG_CHUNK, SG_CHUNK), f32) * SG_CHUNK ** -0.5
    b_s = 1.0 + 0.1 * jax.random.normal(ks[8], (DEPTH, SG_GROUPS, SG_CHUNK), f32)
    w_proj_a = jax.random.normal(ks[9], (DEPTH, HGRN_WIDTH, D_MODEL), f32) * HGRN_WIDTH ** -0.5
    w_proj_b = jax.random.normal(ks[10], (DEPTH, SG_WIDTH, D_MODEL), f32) * SG_WIDTH ** -0.5
    w_out = jax.random.normal(ks[11], (DEPTH, D_MODEL, D_MODEL), f32) * D_MODEL ** -0.5
    final_norm_w = 1.0 + 0.05 * jax.random.normal(ks[12], (D_MODEL,), f32)
    return {"x": x, "norm_w": norm_w, "w_in": w_in, "lower_bounds": lower_bounds,
            "gnorm_w": gnorm_w, "ln_w": ln_w, "ln_b": ln_b, "w_s": w_s, "b_s": b_s,
            "w_proj_a": w_proj_a, "w_proj_b": w_proj_b, "w_out": w_out,
            "final_norm_w": final_norm_w}


def reference(x, norm_w, w_in, lower_bounds, gnorm_w, ln_w, ln_b, w_s, b_s,
              w_proj_a, w_proj_b, w_out, final_norm_w):
    lb = jax.nn.softmax(lower_bounds.astype(jnp.float32), axis=0)
    lb = jnp.cumsum(lb, axis=0) - lb[0]
    split_points = []
    acc = 0
    for size in IN_SPLITS[:-1]:
        acc += size
        split_points.append(acc)
    for layer in range(DEPTH):
        h = _rmsnorm(x, norm_w[layer])
        proj = jnp.einsum('bsd,de->bse', h, w_in[layer])
        (q_raw, ffwd_raw, fbwd_raw, i_raw, ga_raw,
         u_raw, v_raw, gb_raw, ma_raw, mb_raw) = jnp.split(proj, split_points, axis=-1)
        y_a = _hgrn2_branch(q_raw, ffwd_raw, fbwd_raw, i_raw, ga_raw,
                            lb[layer, 0], lb[layer, 1], gnorm_w[layer])
        y_b = _spatial_gating_branch(u_raw, v_raw, gb_raw, ln_w[layer], ln_b[layer],
                                     w_s[layer], b_s[layer])
        merged = (jax.nn.sigmoid(ma_raw) * jnp.einsum('bsw,wd->bsd', y_a, w_proj_a[layer])
                  + jax.nn.sigmoid(mb_raw) * jnp.einsum('bsw,wd->bsd', y_b, w_proj_b[layer]))
        x = x + jnp.einsum('bsd,de->bse', merged, w_out[layer])
    return _rmsnorm(x, final_norm_w)
```

```python
import math
from collections import defaultdict
from contextlib import ExitStack

import numpy as np
import concourse.bass as bass
import concourse.mybir as mybir
from concourse.bass_utils import run_bass_kernel_spmd

F32 = mybir.dt.float32
BF16 = mybir.dt.bfloat16
AF = mybir.ActivationFunctionType
ALU = mybir.AluOpType

D = 1024
KC = 8
H = 8
TC = 512
NT = TC // 128
NSC = TC // 64
MID = 32
RING = 4
NU = 26
R_Q, R_FF, R_FB, R_I, R_GA, R_U, R_V, R_GB, R_MA, R_MB = range(10)
RMS_EPS = 1e-6
LN_EPS = 1e-5

ENGS = ("pe", "act", "dve", "pool", "sp")
EPOCH = 8192
NEPOCH = 8
NDMASEM = 12


class Op:
    __slots__ = ("eng", "fn", "reads", "writes", "dma", "deps", "signal",
                 "sem", "val", "qidx", "waits", "gid", "prev_slot", "phase", "sig")

    def __init__(self, eng, fn, reads, writes, dma):
        self.eng = eng
        self.fn = fn
        self.reads = reads
        self.writes = writes
        self.dma = dma
        self.deps = []
        self.signal = False
        self.sem = None
        self.val = None
        self.waits = []
        self.prev_slot = None
        self.sig = None


class Sched:
    def __init__(self, nc):
        self.nc = nc
        self.ops = []
        self.q = {e: [] for e in ENGS}
        self.last_writer = {}
        self.readers = defaultdict(dict)
        self.phase = ""

    def add(self, eng, fn, reads=(), writes=(), dma=False):
        op = Op(eng, fn, tuple(reads), tuple(writes), dma)
        op.gid = len(self.ops)
        op.phase = self.phase
        op.qidx = len(self.q[eng])
        deps = {}
        for r in op.reads:
            w = self.last_writer.get(r)
            if w is not None:
                deps[w.gid] = w
            if r.startswith("ps:"):
                for key, o in self.readers[r].items():
                    if key != op.eng:
                        deps[o.gid] = o
        for t in op.writes:
            w = self.last_writer.get(t)
            if w is not None:
                deps[w.gid] = w
            for o in self.readers[t].values():
                deps[o.gid] = o
        deps.pop(op.gid, None)
        op.deps = list(deps.values())
        for r in op.reads:
            key = ("d", op.gid) if dma else op.eng
            self.readers[r][key] = op
        for t in op.writes:
            self.last_writer[t] = op
            self.readers[t] = {}
        self.ops.append(op)
        self.q[eng].append(op)
        return op

    def pe(self, fn, reads=(), writes=()):
        return self.add("pe", fn, reads, writes)

    def act(self, fn, reads=(), writes=()):
        return self.add("act", fn, reads, writes)

    def dve(self, fn, reads=(), writes=()):
        return self.add("dve", fn, reads, writes)

    def pool(self, fn, reads=(), writes=()):
        return self.add("pool", fn, reads, writes)

    def dma(self, eng, fn, reads=(), writes=()):
        return self.add(eng, fn, reads, writes, dma=True)

    def _skip(self, d, op):
        if d.dma or op.dma or d.eng != op.eng:
            return False
        if d.eng == "pe":
            return True
        return op.qidx - d.qidx > 2

    def finalize(self, stack, out_dma_ops):
        nc = self.nc
        for op in self.ops:
            for d in op.deps:
                if d.dma or self._skip(d, op):
                    continue
                d.signal = True
        esem = {e: [stack.enter_context(nc.semaphore(f"s_{e}{i}")) for i in range(NEPOCH)] for e in ENGS}
        dsem = {e: [stack.enter_context(nc.semaphore(f"d_{e}{i}")) for i in range(NDMASEM)]
                for e in ("sp", "pool")}
        for e in ENGS:
            nsig = 0
            ndma = 0
            for op in self.q[e]:
                if op.dma:
                    op.sem = dsem[e][ndma % NDMASEM]
                    op.val = 16 * (ndma // NDMASEM + 1)
                    op.prev_slot = (op.sem, op.val - 16) if ndma >= NDMASEM else None
                    ndma += 1
                elif op.signal:
                    ep = nsig // EPOCH
                    assert ep < NEPOCH, f"too many signals on {e}"
                    op.sem = esem[e][ep]
                    op.val = nsig % EPOCH + 1
                    nsig += 1
        for e in ENGS:
            seen = {}
            for op in self.q[e]:
                want = {}
                for d in op.deps:
                    if not d.dma and (not d.signal or self._skip(d, op)):
                        continue
                    k = id(d.sem)
                    if want.get(k, (None, 0))[1] < d.val:
                        want[k] = (d.sem, d.val)
                if op.prev_slot is not None:
                    s, v = op.prev_slot
                    k = id(s)
                    if want.get(k, (None, 0))[1] < v:
                        want[k] = (s, v)
                for k, (s, v) in want.items():
                    if seen.get(k, 0) >= v:
                        continue
                    seen[k] = v
                    op.waits.append((s, v))
        block = stack.enter_context(nc.Block())
        handles = {"pe": block.tensor, "act": block.scalar, "dve": block.vector,
                   "pool": block.gpsimd, "sp": block.sync}

        def make(e):
            def body(eng):
                for op in self.q[e]:
                    for s, v in op.waits:
                        eng.wait_ge(s, v)
                    ins = op.fn(eng)
                    if op.dma:
                        ins.then_inc(op.sem, 16)
                    elif op.signal:
                        ins.then_inc(op.sem, 1)
                if e == "sp":
                    for op in out_dma_ops:
                        eng.wait_ge(op.sem, op.val)
            return body

        for e in ENGS:
            handles[e](make(e))


def v3(ap, b):
    return ap.rearrange("p (a b) -> p a b", b=b)


def build_nc(SEQ, DEPTH):
    NCH = SEQ // TC
    L = DEPTH
    nc = bass.Bass("TRN2", target_bir_lowering=False)

    def din(name, shape):
        return nc.dram_tensor(name, shape, F32, kind="ExternalInput").ap()

    x_d = din("x", [SEQ, D])
    normw_d = din("norm_w", [L, D])
    win_d = din("w_in", [L, D, 10 * D])
    lbs_d = din("lower_bounds", [L, 2, D])
    gnw_d = din("gnorm_w", [L, 128])
    lnw_d = din("ln_w", [L, D])
    lnb_d = din("ln_b", [L, D])
    ws_d = din("w_s", [L, 8, 128, 128])
    bs_d = din("b_s", [L, 8, 128])
    wa_d = din("w_proj_a", [L, D, D])
    wb_d = din("w_proj_b", [L, D, D])
    wo_d = din("w_out", [L, D, D])
    fnw_d = din("final_norm_w", [D])
    out_d = nc.dram_tensor("out", [SEQ, D], F32, kind="ExternalOutput").ap()
    wbd = nc.dram_tensor("wbf", [L, NU, 128, KC * 512], BF16, kind="Internal").ap()
    obd = nc.dram_tensor("obwd", [128, H, SEQ], F32, kind="Internal").ap()
    qsd = nc.dram_tensor("qsilu", [128, H, SEQ], BF16, kind="Internal").ap()
    vtd = nc.dram_tensor("vtok", [SEQ // TC, 128, NT * D], BF16, kind="Internal").ap()

    with ExitStack() as st:
        def sb(name, shape, dt):
            return st.enter_context(nc.sbuf_tensor(name, shape, dt))

        def pst(name, shape, dt=F32):
            return st.enter_context(nc.psum_tensor(name, shape, dt))

        S = Sched(nc)
        out_ops = []

        ident = sb("ident", [128, 128], F32)
        identb = sb("identb", [128, 128], BF16)
        ones = sb("ones", [128, 128], F32)
        onesb = sb("onesb", [128, 128], BF16)
        tri = [sb("tri_f", [128, 128], F32), sb("tri_b", [128, 128], F32)]
        rmask = sb("rmask", [128, TC], F32)
        A = sb("bigA", [128, H, TC], F32)
        B = sb("bigB", [128, H, TC], F32)
        bb = [sb(f"b{i}", [128, H, TC], BF16) for i in range(1, 7)]
        b1, b2, b3, b4, b5, b6 = bb
        xtb = [sb(f"xt{i}", [128, NT, D], F32) for i in range(2)]
        hTb = [sb(f"hT{i}", [128, KC, TC], BF16) for i in range(2)]
        ring = sb("ring", [128, RING, KC, 512], BF16)
        lnw = sb("lnw", [128, D], F32)
        lnb = sb("lnb", [128, D], F32)
        fnw = sb("fnw", [128, D], F32)
        bsrow = sb("bsrow", [1, D], F32)
        wsT = sb("wsT", [128, 8, 128], BF16)
        NTMP = 5
        tmp = [sb(f"tmp{i}", [128, TC], F32) for i in range(NTMP)]
        junk = sb("junk", [128, D], BF16)
        Sst = sb("Sst", [128, H, 128], F32)
        Stl = sb("Stl", [128, H, 128], BF16)
        am = sb("am", [128, H, 128], BF16)
        kTs = sb("kTs", [128, H, 128], BF16)
        args = sb("args", [128, 3, H, NSC], F32)
        sc = sb("sc", [128, 3, H, NSC], F32)
        ssb = sb("ssb", [128, 2, NT], F32)
        lrb = sb("lrb", [128, 2, NT], F32)
        rsb = sb("rsb", [128, 2, NT], F32)
        irsb = sb("irsb", [128, 2, NT], F32)
        ssf = sb("ssf", [128, NT], F32)
        rsf = sb("rsf", [128, NT], F32)
        st6 = sb("st6", [128, NT, 2, 6], F32)
        mv = sb("mv", [128, NT, 2], F32)
        rs2 = sb("rs2", [128, NT], F32)
        nw = sb("nw", [128, L * KC], F32)
        gw = sb("gw", [128, L], F32)
        praw = tmp[0][:, 0:128]
        lbT = tmp[1][:, 0:L * 16].rearrange("p (l x) -> p l x", x=16)
        lbe = tmp[2][:, 0:L * 16].rearrange("p (l x) -> p l x", x=16)
        lbm = sb("lbm", [128, 16], F32)
        lbv = sb("lbv", [128, L, 16], F32)
        oml = sb("oml", [128, L, 16], F32)
        noml = sb("noml", [128, L, 16], F32)
        lnc_t = sb("lnc_t", [128, 1], F32)

        pb = [pst(f"pb{i}", [128, 512], F32) if i != 2 else pst("pb2", [128, 1024], BF16) for i in range(8)]
        bank_first = [True] * 8

        def PS(i):
            return f"ps:{i}"

        tctr = [0]

        def newtmp():
            i = tctr[0] % NTMP
            tctr[0] += 1
            return tmp[i], f"tmp{i}"

        def MM(bank, out, lhsT, rhs, R):
            stt = bank_first[bank]
            bank_first[bank] = False
            op = S.pe(lambda e, o=out, l=lhsT, r=rhs, a=stt: e.matmul(o, lhsT=l, rhs=r, start=a, stop=False,
                                                                      skip_group_check=True),
                      reads=R, writes=[PS(bank)])
            op.sig = ("mm", lhsT.shape[0], int(np.prod(rhs.shape[1:])), str(rhs.dtype))

        def TR(bank, out, in_, idn, R):
            bank_first[bank] = False
            op = S.pe(lambda e, o=out, i=in_, d=idn: e.transpose(o, i, d), reads=R, writes=[PS(bank)])
            op.sig = ("tr", in_.shape[0], int(np.prod(in_.shape[1:])), str(in_.dtype))

        def release(bank):
            bank_first[bank] = True

        def ACT(out, in_, func, R, W, scale=1.0, bias=None, accum=None):
            kw = {}
            if bias is not None:
                kw["bias"] = bias
            if accum is not None:
                kw["accum_out"] = accum
            S.act(lambda e, o=out, i=in_, f=func, s=scale, kw=kw: e.activation(out=o, in_=i, func=f, scale=s, **kw),
                  reads=R, writes=W)

        def TT(eng, out, in0, in1, op, R, W):
            S.add(eng, lambda e, o=out, a=in0, b=in1, p=op: e.tensor_tensor(out=o, in0=a, in1=b, op=p), reads=R, writes=W)

        def TS(eng, out, in0, s1, s2, op0, op1, R, W):
            if s2 is None:
                S.add(eng, lambda e, o=out, a=in0, x=s1, p=op0:
                      e.tensor_scalar(out=o, in0=a, scalar1=x, scalar2=None, op0=p), reads=R, writes=W)
            else:
                S.add(eng, lambda e, o=out, a=in0, x=s1, y=s2, p=op0, q=op1:
                      e.tensor_scalar(out=o, in0=a, scalar1=x, scalar2=y, op0=p, op1=q), reads=R, writes=W)

        def STT(out, in0, scalar, in1, op0, op1, R, W):
            S.dve(lambda e, o=out, a=in0, s=scalar, b=in1, p=op0, q=op1:
                  e.scalar_tensor_tensor(out=o, in0=a, scalar=s, in1=b, op0=p, op1=q), reads=R, writes=W)

        def CP(eng, out, in_, R, W):
            S.add(eng, lambda e, o=out, i=in_: e.tensor_copy(out=o, in_=i), reads=R, writes=W)

        def MEMSET(eng, out, val, W):
            S.add(eng, lambda e, o=out, v=val: e.memset(o, v), writes=W)

        def DMA(q, out, in_, R, W):
            return S.dma(q, lambda e, o=out, i=in_: e.dma_start(out=o, in_=i), reads=R, writes=W)

        def hn(base, hs=range(H)):
            return [f"{base}:{h}" for h in hs]

        def wsrc(l, u):
            if u < 20:
                src = win_d[l, :, u * 512:(u + 1) * 512]
            elif u < 22:
                src = wa_d[l, :, (u - 20) * 512:(u - 19) * 512]
            elif u < 24:
                src = wb_d[l, :, (u - 22) * 512:(u - 21) * 512]
            else:
                src = wo_d[l, :, (u - 24) * 512:(u - 23) * 512]
            return src.rearrange("(kc p) c -> p kc c", p=128)

        def cast_unit(l, u):
            DMA("pool", wbd[l, u].rearrange("p (kc c) -> p kc c", c=512), wsrc(l, u), [], [f"wbd:{l}:{u}"])

        U_WA, U_WB, U_WO = 20, 22, 24

        def pass_units(l, pas):
            if pas == 1:
                ul = [2 * r + k for r in (R_FB, R_Q, R_I) for k in range(2)]
            else:
                ul = [2 * r + k for r in (R_FF, R_GA, R_U, R_V, R_GB, R_MA) for k in range(2)]
                ul += [U_WA, U_WA + 1, 2 * R_MB, 2 * R_MB + 1, U_WB, U_WB + 1, U_WO, U_WO + 1]
            return [(l, u) for u in ul]

        plan = []
        for l in range(L):
            for pas in (1, 2):
                for _ in range(NCH):
                    plan += pass_units(l, pas)
        wstate = {"ptr": 0, "issued": 0}

        def w_issue_upto(n):
            while wstate["issued"] < min(len(plan), n):
                i = wstate["issued"]
                l, u = plan[i]
                slot = i % RING
                DMA("sp", ring[:, slot].rearrange("p kc c -> p (kc c)"), wbd[l, u], [f"wbd:{l}:{u}"], [f"w:{slot}"])
                wstate["issued"] += 1

        def wunit(l, u):
            p = wstate["ptr"]
            assert plan[p] == (l, u), (plan[p], l, u)
            w_issue_upto(p + RING)
            wstate["ptr"] += 1
            slot = p % RING
            return slot, [f"w:{slot}"]

        def cast_order(l):
            first = [u for (_, u) in pass_units(l, 1)]
            rest = [u for (_, u) in pass_units(l, 2) if u not in first]
            rest += [u for u in range(NU) if u not in first and u not in rest and not (2 * R_FB <= u < 2 * R_FB + 2)]
            return first + rest

        assert sorted(cast_order(0)) == list(range(NU)), cast_order(0)
        cast_q = []

        def emit_casts(n):
            for _ in range(n):
                if cast_q:
                    l, u = cast_q.pop(0)
                    cast_unit(l, u)

        MEMSET("pool", ident[:], 1.0, ["ident"])
        S.pool(lambda e: e.affine_select(out=ident[:], in_=ident[:], pattern=[[-1, 128]], compare_op=ALU.is_equal,
                                         fill=0.0, base=0, channel_multiplier=1), reads=["ident"], writes=["ident"])
        CP("pool", identb[:], ident[:], ["ident"], ["identb"])
        MEMSET("pool", ones[:], 1.0, ["ones"])
        MEMSET("pool", onesb[:], 1.0, ["onesb"])
        MEMSET("pool", tri[0][:], 1.0, ["tri0"])
        S.pool(lambda e: e.affine_select(out=tri[0][:], in_=tri[0][:], pattern=[[1, 128]], compare_op=ALU.is_ge,
                                         fill=0.0, base=0, channel_multiplier=-1), reads=["tri0"], writes=["tri0"])
        MEMSET("pool", tri[0][0:64, 64:128], 0.0, ["tri0"])
        MEMSET("pool", tri[1][:], 1.0, ["tri1"])
        S.pool(lambda e: e.affine_select(out=tri[1][:], in_=tri[1][:], pattern=[[-1, 128]], compare_op=ALU.is_ge,
                                         fill=0.0, base=0, channel_multiplier=1), reads=["tri1"], writes=["tri1"])
        MEMSET("pool", tri[1][64:128, 0:64], 0.0, ["tri1"])
        MEMSET("pool", rmask[:], 1.0, ["rmask"])
        MEMSET("pool", v3(rmask[:], 64)[:, :, 0:1], 0.0, ["rmask"])
        LNC = math.log(1.0 / math.sqrt(128.0))
        MEMSET("pool", lnc_t[:], LNC, ["lnc"])

        for u in cast_order(0):
            cast_unit(0, u)

        def load_T(dst, src2d, nrows):
            DMA("sp", praw[0:nrows, :], src2d, [], ["tmp0"])
            TR(7, pb[7][:, 0:nrows], praw[0:nrows, :], ident[0:nrows, 0:nrows], ["tmp0", "ident"])
            CP("dve", dst, pb[7][:, 0:nrows], [PS(7)], ["params"])
            release(7)

        load_T(nw[:], normw_d.rearrange("l (kc p) -> (l kc) p", p=128), L * KC)
        load_T(gw[:], gnw_d, L)
        load_T(lbT.rearrange("p l x -> p (l x)"), lbs_d.rearrange("l d (h p) -> (l d h) p", p=128), L * 16)
        DMA("sp", fnw[:], fnw_d.partition_broadcast(128), [], ["fnw"])
        CP("dve", lbm[:], lbT[:, 0, :], ["params", "tmp1"], ["lbm"])
        for l in range(1, L):
            TT("dve", lbm[:], lbm[:], lbT[:, l, :], ALU.max, ["params", "tmp1", "lbm"], ["lbm"])
        for l in range(L):
            TT("dve", lbe[:, l, :], lbT[:, l, :], lbm[:], ALU.subtract, ["params", "tmp1", "lbm"], ["tmp2"])
        ACT(lbe.rearrange("p l x -> p (l x)"), lbe.rearrange("p l x -> p (l x)"), AF.Exp, ["tmp2"], ["tmp2"])
        CP("dve", lbm[:], lbe[:, 0, :], ["tmp2"], ["lbm"])
        for l in range(1, L):
            TT("dve", lbm[:], lbm[:], lbe[:, l, :], ALU.add, ["tmp2", "lbm"], ["lbm"])
        S.dve(lambda e: e.reciprocal(out=lbm[:], in_=lbm[:]), reads=["lbm"], writes=["lbm"])
        for l in range(L):
            TT("dve", lbe[:, l, :], lbe[:, l, :], lbm[:], ALU.mult, ["tmp2", "lbm"], ["tmp2"])
        MEMSET("dve", lbv[:, 0, :], 0.0, ["lbv"])
        for l in range(1, L):
            TT("dve", lbv[:, l, :], lbv[:, l - 1, :], lbe[:, l, :], ALU.add, ["tmp2", "lbv"], ["lbv"])
        TS("dve", oml[:].rearrange("p l x -> p (l x)"), lbv[:].rearrange("p l x -> p (l x)"), -1.0, 1.0, ALU.mult, ALU.add,
           ["lbv"], ["lbc"])
        TS("dve", noml[:].rearrange("p l x -> p (l x)"), lbv[:].rearrange("p l x -> p (l x)"), 1.0, -1.0, ALU.mult, ALU.add,
           ["lbv"], ["lbc"])

        steps = []
        for l in range(L):
            for j in range(NCH - 1, -1, -1):
                steps.append((l, 1, j))
            for j in range(NCH):
                steps.append((l, 2, j))
        sbuf_of = []
        share = []
        for i, (l, pas, j) in enumerate(steps):
            if i == 0:
                sbuf_of.append(0); share.append("none")
            else:
                pl, pp, pj = steps[i - 1]
                if pl == l and pp == 1 and pas == 2 and pj == j:
                    sbuf_of.append(sbuf_of[-1]); share.append("same")
                elif pp == 2 and pas == 1 and pl + 1 == l and pj == j:
                    sbuf_of.append(sbuf_of[-1]); share.append("carry")
                else:
                    sbuf_of.append(1 - sbuf_of[-1]); share.append("none")

        def hTn(bf):
            return [f"hT{bf}:{kc}" for kc in range(KC)]

        def load_x(l, j, bf):
            src = x_d if l == 0 else out_d
            DMA("sp", xtb[bf][:], src[j * TC:(j + 1) * TC, :].rearrange("(t p) d -> p t d", p=128),
                [f"xd:{j}"] if l > 0 else [], [f"xt{bf}"])

        def rms_stats(bf, do_scale=True):
            xt = xtb[bf]
            xn = f"xt{bf}"
            for tt in range(NT):
                ACT(junk[:], xt[:, tt, :], AF.Square, [xn], ["junk", f"ss{bf}"], accum=ssb[:, bf, tt:tt + 1])
            ACT(lrb[:, bf, :], ssb[:, bf, :], AF.Ln, [f"ss{bf}"], [f"lr{bf}"], scale=1.0 / D, bias=RMS_EPS)
            ACT(rsb[:, bf, :], lrb[:, bf, :], AF.Exp, [f"lr{bf}"], [f"rs{bf}"], scale=-0.5)
            ACT(irsb[:, bf, :], lrb[:, bf, :], AF.Exp, [f"lr{bf}"], [f"irs{bf}"], scale=0.5)
            if do_scale:
                rms_scale(bf)

        def rms_scale(bf):
            xt = xtb[bf]
            xn = f"xt{bf}"
            for tt in range(NT):
                TS("dve", xt[:, tt, :], xt[:, tt, :], rsb[:, bf, tt:tt + 1], None, ALU.mult, ALU.bypass,
                   [xn, f"rs{bf}"], [xn])

        def rms_transposes(l, bf, banks):
            xt = xtb[bf]
            xn = f"xt{bf}"
            for kc in range(KC):
                bank = banks[kc % len(banks)]
                for tt in range(NT):
                    TR(bank, pb[bank][:, tt * 128:(tt + 1) * 128], xt[:, tt, kc * 128:(kc + 1) * 128], ident[:],
                       [xn, "ident"])
                ACT(hTb[bf][:, kc, :], pb[bank][:], AF.Copy, [PS(bank), "params"], [f"hT{bf}:{kc}"],
                    scale=nw[:, l * KC + kc:l * KC + kc + 1])
                release(bank)

        def prep_load(i):
            if i < len(steps) and share[i] == "none":
                l, pas, j = steps[i]
                load_x(l, j, sbuf_of[i])

        def prep_stats(i, do_scale=True):
            if i < len(steps) and share[i] != "same":
                rms_stats(sbuf_of[i], do_scale)

        def prep_tr(i, banks):
            if i < len(steps) and share[i] != "same":
                rms_transposes(steps[i][0], sbuf_of[i], banks)

        FM_BANKS = (0, 1, 3, 4)

        def proj_fm_unit(l, u, bf, evac):
            slot, wn = wunit(l, u)
            for jj in range(4):
                bank = FM_BANKS[jj]
                for kc in range(KC):
                    MM(bank, pb[bank][:], ring[:, slot, kc, jj * 128:(jj + 1) * 128], hTb[bf][:, kc, :], wn + [f"hT{bf}:{kc}"])
                evac(jj, bank)
                release(bank)

        def proj_fm_region(l, reg, bf, evac):
            for k in range(2):
                proj_fm_unit(l, 2 * reg + k, bf, lambda jj, bank, k=k: evac(4 * k + jj, bank))

        def proj_tm(l, u, rhs_src, rhs_names, evac):
            slot, wn = wunit(l, u)
            for tt in range(NT):
                bank = 3 + (tt % 2)
                for kc in range(KC):
                    MM(bank, pb[bank][:], rhs_src[:, kc, tt * 128:(tt + 1) * 128], ring[:, slot, kc, :],
                       wn + [rhs_names[kc]])
                evac(tt, bank)
                release(bank)

        Bf = B[:].rearrange("p h t -> p (h t)")
        Af = A[:].rearrange("p h t -> p (h t)")
        b1f = b1[:].rearrange("p h t -> p (h t)")
        b2f = b2[:].rearrange("p h t -> p (h t)")
        b3f = b3[:].rearrange("p h t -> p (h t)")
        b6f = b6[:].rearrange("p h t -> p (h t)")

        def gate_proj(l, dirn, bf, after_first=None):
            reg = R_FF if dirn == 0 else R_FB

            def evac(h, bank):
                t, tn = newtmp()
                ACT(t[:], pb[bank][:], AF.Sigmoid, [PS(bank)], [tn])
                col = dirn * 8 + h
                TS("dve", B[:, h, :], t[:], oml[:, l, col:col + 1], lbv[:, l, col:col + 1], ALU.mult, ALU.add,
                   [tn, "lbc", "lbv"], [f"B:{h}"])
                TS("dve", b3[:, h, :], t[:], noml[:, l, col:col + 1], oml[:, l, col:col + 1], ALU.mult, ALU.add,
                   [tn, "lbc"], [f"b3:{h}"])
            proj_fm_unit(l, 2 * reg, bf, lambda jj, bank: evac(jj, bank))
            if after_first is not None:
                after_first()
            proj_fm_unit(l, 2 * reg + 1, bf, lambda jj, bank: evac(4 + jj, bank))

        def gate_chain1(dirn):
            ACT(Bf, Bf, AF.Ln, hn("B"), hn("B"))
            for h in range(H):
                S.dve(lambda e, h=h: e.tensor_tensor_scan(out=A[:, h, :], data0=rmask[:], data1=B[:, h, :], initial=0.0,
                                                          op0=ALU.mult, op1=ALU.add),
                      reads=["rmask", f"B:{h}"], writes=[f"A:{h}"])
            A4 = A[:].rearrange("p h (c t) -> p h c t", t=64)
            B4 = B[:].rearrange("p h (c t) -> p h c t", t=64)
            A3 = Af.rearrange("p (g t) -> p g t", t=64)
            B3 = Bf.rearrange("p (g t) -> p g t", t=64)
            if dirn == 0:
                CP("pool", args[:, 0], A4[:, :, :, MID], hn("A"), ["args"])
                CP("pool", args[:, 1], A4[:, :, :, 63], hn("A"), ["args"])
                TT("pool", args[:, 2], A4[:, :, :, 63], A4[:, :, :, MID], ALU.subtract, hn("A"), ["args"])
                TT("dve", B3, A3, A3[:, :, MID:MID + 1].to_broadcast([128, H * NSC, 64]), ALU.subtract, hn("A"), hn("B"))
            else:
                TT("dve", Bf, Af, Bf, ALU.subtract, hn("A") + hn("B"), hn("B"))
                TT("pool", args[:, 0], A4[:, :, :, 63], B4[:, :, :, MID], ALU.subtract, hn("A") + hn("B"), ["args"])
                CP("pool", args[:, 1], A4[:, :, :, 63], hn("A"), ["args"])
                CP("pool", args[:, 2], B4[:, :, :, MID], hn("B"), ["args"])
                TT("dve", A3, B3, B3[:, :, MID:MID + 1].to_broadcast([128, H * NSC, 64]), ALU.subtract,
                   hn("B") + ["args"], hn("A"))

        def chain2_sc():
            ACT(sc[:].rearrange("p a h c -> p (a h c)"), args[:].rearrange("p a h c -> p (a h c)"), AF.Exp, ["args"], ["sc"])

        def chain2_head(dirn, h):
            if dirn == 0:
                Dn, Dh, sgn = f"B:{h}", B[:, h, :], 1.0
            else:
                Dn, Dh, sgn = f"A:{h}", A[:, h, :], -1.0
            ACT(b6[:, h, :], Dh, AF.Exp, [Dn, "lnc"], [f"b6:{h}"], scale=sgn, bias=lnc_t[:, 0:1])
            ACT(b2[:, h, :], Dh, AF.Exp, [Dn], [f"b2:{h}"], scale=-sgn)
            TT("dve", b2[:, h, :], b2[:, h, :], b3[:, h, :], ALU.mult, [f"b2:{h}", f"b3:{h}"], [f"b2:{h}"])
            TT("dve", b1[:, h, :], b1[:, h, :], b6[:, h, :], ALU.mult, [f"b1:{h}", f"b6:{h}"], [f"b1:{h}"])
            TT("pool", v3(b3[:, h, :], 64), v3(b2[:, h, :], 64),
               sc[:, 2, h, :].unsqueeze(2).to_broadcast([128, NSC, 64]), ALU.mult, [f"b2:{h}", "sc"], [f"b3:{h}"])

        def q_phase(l, bf):
            proj_fm_region(l, R_Q, bf, lambda h, bank: ACT(b1[:, h, :], pb[bank][:], AF.Silu, [PS(bank)], [f"b1:{h}"]))

        def ga_phase(l, bf):
            proj_fm_region(l, R_GA, bf, lambda h, bank: ACT(b5[:, h, :], pb[bank][:], AF.Silu, [PS(bank)], [f"b5:{h}"]))

        b4v = b4[:].rearrange("p h t -> p (h t)").rearrange("p (t d) -> p t d", d=D)
        b3v = b3[:].rearrange("p h t -> p (h t)").rearrange("p (t d) -> p t d", d=D)

        def i_phase(l, bf, dirn):
            chain2_sc()
            for u2 in range(2):
                def evac(tt, bank, u2=u2):
                    ACT(b4v[:, tt, u2 * 512:(u2 + 1) * 512], pb[bank][:], AF.Copy, [PS(bank)], hn("b4"))
                    chain2_head(dirn, 4 * u2 + tt)
                proj_tm(l, 2 * R_I + u2, hTb[bf], hTn(bf), evac)

        def gla(dirn, casts=0):
            tiles = range(NT) if dirn == 0 else range(NT - 1, -1, -1)
            GR = (range(0, 4), range(4, 8))
            for tt in tiles:
                cols = slice(tt * 128, (tt + 1) * 128)
                for g in range(2):
                    hs = GR[g]
                    for h in hs:
                        MM(g, pb[g][:, (h % 4) * 128:(h % 4 + 1) * 128], b2[:, h, cols], b1[:, h, cols],
                           [f"b2:{h}", f"b1:{h}"])
                    TT("dve", am[:, g * 4:(g + 1) * 4, :], v3(pb[g][:], 128),
                       tri[dirn][:].unsqueeze(1).to_broadcast([128, 4, 128]), ALU.mult,
                       [PS(g), f"tri{dirn}"], [f"am{g}"])
                    release(g)
                    kb = 2 if g == 0 else 7
                    kview = pb[2][:, 0:512] if g == 0 else pb[7][:].bitcast(BF16)[:, 0:512]
                    for h in hs:
                        TR(kb, kview[:, (h % 4) * 128:(h % 4 + 1) * 128], b3[:, h, cols], identb[:], [f"b3:{h}", "identb"])
                    ACT(kTs[:, g * 4:(g + 1) * 4, :], v3(kview, 128), AF.Copy, [PS(kb)], [f"kTs{g}"])
                    release(kb)
                    for h in hs:
                        MM(3 + g, pb[3 + g][:, (h % 4) * 128:(h % 4 + 1) * 128], b4v[:, tt, h * 128:(h + 1) * 128], am[:, h, :],
                           hn("b4") + [f"am{g}"])
                subs = (2 * tt, 2 * tt + 1) if dirn == 0 else (2 * tt + 1, 2 * tt)
                for c in subs:
                    half = c % 2
                    rows = slice(half * 64, half * 64 + 64)
                    ccols = slice(c * 64, c * 64 + 64)
                    for g in range(2):
                        hs = GR[g]
                        e1 = "dve" if g == 0 else "pool"
                        hsl = slice(g * 4, (g + 1) * 4)
                        TT(e1, Stl[:, hsl, :], Sst[:, hsl, :], sc[:, 0, hsl, c:c + 1].to_broadcast([128, 4, 128]), ALU.mult,
                           [f"Sst{g}", "sc"], [f"Stl{g}"])
                        for h in hs:
                            o0 = (h % 4) * 128 + half * 64
                            MM(3 + g, pb[3 + g][:, o0:o0 + 64], Stl[:, h, :], b1[:, h, ccols], [f"Stl{g}", f"b1:{h}"])
                        for h in hs:
                            MM(5 + g, pb[5 + g][:, (h % 4) * 128:(h % 4 + 1) * 128], kTs[rows, h, :],
                               b4v[rows, tt, h * 128:(h + 1) * 128], [f"kTs{g}"] + hn("b4"))
                        TT(e1, Sst[:, hsl, :], Sst[:, hsl, :], sc[:, 1, hsl, c:c + 1].to_broadcast([128, 4, 128]), ALU.mult,
                           [f"Sst{g}", "sc"], [f"Sst{g}"])
                        TT("dve", Sst[:, hsl, :], Sst[:, hsl, :], v3(pb[5 + g][:], 128), ALU.add, [f"Sst{g}", PS(5 + g)],
                           [f"Sst{g}"])
                        release(5 + g)
                for g in range(2):
                    ACT(B[:, g * 4:(g + 1) * 4, cols], v3(pb[3 + g][:], 128), AF.Copy, [PS(3 + g)],
                        [f"B:{h}" for h in GR[g]])
                    release(3 + g)
                emit_casts(casts)

        def layer_setup(l):
            DMA("sp", lnw[:], lnw_d[l].partition_broadcast(128), [], ["lnw"])
            DMA("sp", lnb[:], lnb_d[l].partition_broadcast(128), [], ["lnb"])
            DMA("sp", bsrow[:], bs_d[l].rearrange("g t -> (g t)").unsqueeze(0), [], ["bsrow"])
            wstage = A[:, 0:2, :].rearrange("p h t -> p (h t)").rearrange("p (g s) -> p g s", s=128)
            DMA("sp", wstage, ws_d[l].rearrange("g t s -> t g s"), [], hn("A"))
            for g in range(8):
                bk = 5 + g // 4
                TR(bk, pb[bk][:, (g % 4) * 128:(g % 4 + 1) * 128], wstage[:, g, :], ident[:], hn("A") + ["ident"])
            for bk in range(2):
                CP("dve", wsT[:, bk * 4:(bk + 1) * 4, :], v3(pb[5 + bk][:], 128), [PS(5 + bk)], ["wsT"])
                release(5 + bk)

        def chunk_pass1(i):
            l, _, j = steps[i]
            bf = sbuf_of[i]
            nxt = i + 1 if (i + 1 < len(steps) and share[i + 1] == "none") else None
            S.phase = "p1.gate"
            if nxt is not None:
                prep_load(nxt)
            gate_proj(l, 1, bf)
            gate_chain1(1)
            S.phase = "p1.q"
            q_phase(l, bf)
            DMA("sp", qsd[:, :, j * TC:(j + 1) * TC], b1[:], hn("b1"), [f"qs:{j}"])
            S.phase = "p1.st"
            if nxt is not None:
                prep_stats(nxt)
            S.phase = "p1.i"
            i_phase(l, bf, 1)
            DMA("sp", vtd[j], b4[:].rearrange("p h t -> p (h t)"), hn("b4"), [f"vt:{j}"])
            S.phase = "p1.prep"
            if nxt is not None:
                prep_tr(nxt, (5, 6))
            S.phase = "p1.gla"
            gla(1)
            DMA("sp", obd[:, :, j * TC:(j + 1) * TC], B[:], hn("B"), [f"ob:{j}"])

        def chunk_pass2(i, last):
            l, _, j = steps[i]
            bf = sbuf_of[i]
            xt = xtb[bf]
            xn = f"xt{bf}"
            ncast = 1 if l + 1 < L else 0
            nxt = i + 1 if (i + 1 < len(steps) and share[i + 1] == "none") else None
            S.phase = "p2.gate"
            DMA("sp", b1[:], qsd[:, :, j * TC:(j + 1) * TC], [f"qs:{j}"], hn("b1"))
            DMA("sp", b4[:].rearrange("p h t -> p (h t)"), vtd[j], [f"vt:{j}"], hn("b4"))
            if nxt is not None:
                prep_load(nxt)
            gate_proj(l, 0, bf)
            gate_chain1(0)
            emit_casts(ncast)
            emit_casts(ncast)
            S.phase = "p2.ga"
            ga_phase(l, bf)
            emit_casts(ncast)
            S.phase = "p2.i"
            chain2_sc()
            for h in range(H):
                chain2_head(0, h)
            emit_casts(ncast)
            DMA("sp", A[:], obd[:, :, j * TC:(j + 1) * TC], [f"ob:{j}"], hn("A"))
            S.phase = "p2.gla"
            gla(0, ncast)
            S.phase = "p2.hnorm"
            TT("dve", Bf, Bf, Af, ALU.add, hn("A") + hn("B"), hn("B"))
            ACT(b3f, Bf, AF.Square, hn("B"), hn("b3"))
            S.phase = "p2.u"
            proj_fm_region(l, R_U, bf,
                           lambda g, bank: ACT(b2[:, g, :], pb[bank][:], AF.Gelu_apprx_tanh, [PS(bank)], [f"b2:{g}"]))
            S.phase = "p2.hnorm"
            for h in range(H):
                bank = 5 + h % 3
                MM(bank, pb[bank][:], onesb[:], b3[:, h, :], ["onesb", f"b3:{h}"])
                r, rn = newtmp()
                ACT(r[:], pb[bank][:], AF.Ln, [PS(bank)], [rn], scale=1.0 / 128, bias=RMS_EPS)
                release(bank)
                ACT(r[:], r[:], AF.Exp, [rn], [rn], scale=-0.5)
                TT("dve", r[:], r[:], B[:, h, :], ALU.mult, [rn, f"B:{h}"], [rn])
                STT(b1[:, h, :], r[:], gw[:, l:l + 1], b5[:, h, :], ALU.mult, ALU.mult, [rn, "params", f"b5:{h}"],
                    [f"b1:{h}"])
            emit_casts(ncast)
            S.phase = "p2.v"
            Av = Af.rearrange("p (t d) -> p t d", d=D)
            for u2 in range(2):
                def evac(tt, bank, u2=u2):
                    ACT(Av[:, tt, u2 * 512:(u2 + 1) * 512], pb[bank][:], AF.Gelu_apprx_tanh, [PS(bank)], hn("A"))
                proj_tm(l, 2 * R_V + u2, hTb[bf], hTn(bf), evac)
            emit_casts(ncast)
            S.phase = "p2.lns"
            for tt in range(NT):
                for k in range(2):
                    S.dve(lambda e, tt=tt, k=k: e.bn_stats(out=st6[:, tt, k, :], in_=Av[:, tt, k * 512:(k + 1) * 512]),
                          reads=hn("A"), writes=["st6"])
                S.dve(lambda e, tt=tt: e.bn_aggr(out=mv[:, tt, :], in_=st6[:, tt].rearrange("p a b -> p (a b)")),
                      reads=["st6"], writes=["mv"])
            S.phase = "p2.gb"
            proj_fm_region(l, R_GB, bf, lambda g, bank: ACT(b6[:, g, :], pb[bank][:], AF.Silu, [PS(bank)], [f"b6:{g}"]))
            TT("dve", b2f, b2f, b6f, ALU.mult, hn("b2") + hn("b6"), hn("b2"))
            S.phase = "p2.ln"
            ACT(rs2[:], mv[:, :, 1], AF.Ln, ["mv"], ["rs2"], bias=LN_EPS)
            ACT(rs2[:], rs2[:], AF.Exp, ["rs2"], ["rs2"], scale=-0.5)
            for tt in range(NT):
                TS("dve", Av[:, tt, :], Av[:, tt, :], mv[:, tt, 0:1], rs2[:, tt:tt + 1], ALU.subtract, ALU.mult,
                   hn("A") + ["mv", "rs2"], [f"Av:{tt}"])
            for tt in range(NT):
                TT("pool", Av[:, tt, :], Av[:, tt, :], lnw[:], ALU.mult, [f"Av:{tt}", "lnw"], [f"Av:{tt}"])
            for tt in range(NT):
                TT("dve", b3v[:, tt, :], Av[:, tt, :], lnb[:], ALU.add, [f"Av:{tt}", "lnb"], hn("b3") + hn("A"))
            emit_casts(ncast)
            S.phase = "p2.pa"
            proj_fm_region(l, R_MA, bf, lambda e_, bank: ACT(b6[:, e_, :], pb[bank][:], AF.Sigmoid, [PS(bank)], [f"b6:{e_}"]))
            S.phase = "p2.st"
            if nxt is not None:
                prep_stats(nxt, do_scale=False)
            S.phase = "p2.pa"
            for k in range(2):
                slot, wn = wunit(l, U_WA + k)
                for jj in range(4):
                    e_ = 4 * k + jj
                    bank = FM_BANKS[jj]
                    for kc in range(KC):
                        MM(bank, pb[bank][:], ring[:, slot, kc, jj * 128:(jj + 1) * 128], b1[:, kc, :], wn + [f"b1:{kc}"])
                    TT("dve", B[:, e_, :], pb[bank][:], b6[:, e_, :], ALU.mult, [PS(bank), f"b6:{e_}"], [f"B:{e_}"])
                    release(bank)
            emit_casts(ncast)
            S.phase = "p2.prep"
            if nxt is not None:
                rms_scale(sbuf_of[nxt])
                prep_tr(nxt, (5, 3))
            S.phase = "p2.mix"
            for g in range(8):
                mb = 7 if g % 2 == 0 else 6
                for tt in range(NT):
                    MM(mb, pb[mb][:, tt * 128:(tt + 1) * 128], b3v[:, tt, g * 128:(g + 1) * 128], wsT[:, g, :],
                       hn("b3") + ["wsT"])
                MM(mb, v3(pb[mb][:], 128), ones[0:1, :], bsrow[0:1, g * 128:(g + 1) * 128].unsqueeze(1).to_broadcast([1, NT, 128]),
                   ["ones", "bsrow"])
                TT("dve", b4[:, g, :], pb[mb][:], b2[:, g, :], ALU.mult, [PS(mb), f"b2:{g}"], [f"b4:{g}"])
                release(mb)
            emit_casts(ncast)
            S.phase = "p2.pb"
            proj_fm_region(l, R_MB, bf, lambda e_, bank: ACT(b6[:, e_, :], pb[bank][:], AF.Sigmoid, [PS(bank)], [f"b6:{e_}"]))
            for k in range(2):
                slot, wn = wunit(l, U_WB + k)
                for jj in range(4):
                    e_ = 4 * k + jj
                    bank = FM_BANKS[jj]
                    for kc in range(KC):
                        MM(bank, pb[bank][:], ring[:, slot, kc, jj * 128:(jj + 1) * 128], b4[:, kc, :], wn + [f"b4:{kc}"])
                    sg, sgn_ = newtmp()
                    TT("dve", sg[:], pb[bank][:], b6[:, e_, :], ALU.mult, [PS(bank), f"b6:{e_}"], [sgn_])
                    release(bank)
                    TT("dve", b5[:, e_, :], sg[:], B[:, e_, :], ALU.add, [sgn_, f"B:{e_}"], [f"b5:{e_}"])
            emit_casts(ncast)
            S.phase = "p2.out"
            b5n = [f"b5:{kc}" for kc in range(KC)]
            for u2 in range(2):
                def evac(tt, bank, u2=u2):
                    STT(xt[:, tt, u2 * 512:(u2 + 1) * 512], xt[:, tt, u2 * 512:(u2 + 1) * 512], irsb[:, bf, tt:tt + 1],
                        pb[bank][:], ALU.mult, ALU.add, [PS(bank), xn, f"irs{bf}"], [xn])
                proj_tm(l, U_WO + u2, b5, b5n, evac)
            if last:
                for tt in range(NT):
                    ACT(junk[:], xt[:, tt, :], AF.Square, [xn], ["junk", "ssf"], accum=ssf[:, tt:tt + 1])
                ACT(rsf[:], ssf[:], AF.Ln, ["ssf"], ["rsf"], scale=1.0 / D, bias=RMS_EPS)
                ACT(rsf[:], rsf[:], AF.Exp, ["rsf"], ["rsf"], scale=-0.5)
                for tt in range(NT):
                    STT(xt[:, tt, :], xt[:, tt, :], rsf[:, tt:tt + 1], fnw[:], ALU.mult, ALU.mult, [xn, "rsf", "fnw"], [xn])
            op = DMA("sp", out_d[j * TC:(j + 1) * TC, :].rearrange("(t p) d -> p t d", p=128), xt[:], [xn], [f"xd:{j}"])
            if last:
                out_ops.append(op)
            emit_casts(ncast)

        for i, (l, pas, j) in enumerate(steps):
            if pas == 1 and j == NCH - 1:
                layer_setup(l)
                MEMSET("pool", Sst[:], 0.0, ["Sst0", "Sst1"])
                if l + 1 < L:
                    cast_q.extend((l + 1, g) for g in cast_order(l + 1))
                S.phase = "prep0"
                prep_load(i)
                prep_stats(i)
                prep_tr(i, (5, 6))
            if pas == 2 and j == 0:
                MEMSET("pool", Sst[:], 0.0, ["Sst0", "Sst1"])
            if pas == 1:
                chunk_pass1(i)
            else:
                chunk_pass2(i, l == L - 1)
            if pas == 2 and j == NCH - 1:
                emit_casts(len(cast_q))
        assert wstate["ptr"] == len(plan)
        assert not cast_q
        S.finalize(st, out_ops)
        global LAST_SCHED
        LAST_SCHED = S
    return nc


LAST_SCHED = None
_NC_CACHE = {}


def _get_nc(seq, depth):
    key = (seq, depth)
    if key not in _NC_CACHE:
        _NC_CACHE[key] = build_nc(seq, depth)
    return _NC_CACHE[key]


def kernel(x, norm_w, w_in, lower_bounds, gnorm_w, ln_w, ln_b, w_s, b_s, w_proj_a, w_proj_b, w_out, final_norm_w):
    x = np.asarray(x, dtype=np.float32)
    Bn, SEQ, _ = x.shape
    DEPTH = norm_w.shape[0]
    nc = _get_nc(SEQ, DEPTH)
    shared = {
        "norm_w": norm_w, "w_in": w_in, "lower_bounds": lower_bounds, "gnorm_w": gnorm_w, "ln_w": ln_w, "ln_b": ln_b,
        "w_s": w_s, "b_s": b_s, "w_proj_a": w_proj_a, "w_proj_b": w_proj_b, "w_out": w_out, "final_norm_w": final_norm_w,
    }
    shared = {k: np.ascontiguousarray(np.asarray(v, dtype=np.float32)) for k, v in shared.items()}
    in_maps = [dict(shared, x=np.ascontiguousarray(x[b])) for b in range(Bn)]
    res = run_bass_kernel_spmd(nc, in_maps, core_ids=list(range(Bn)))
    return np.stack([np.asarray(r["out"]) for r in res.results], axis=0).astype(np.float32)
```

```python
import math
from collections import defaultdict
from contextlib import ExitStack

import numpy as np
import concourse.bass as bass
import concourse.mybir as mybir
from concourse.bass_utils import run_bass_kernel_spmd

F32 = mybir.dt.float32
BF16 = mybir.dt.bfloat16
AF = mybir.ActivationFunctionType
ALU = mybir.AluOpType

D = 1024
KC = 8
H = 8
TC = 512
NT = TC // 128
NSC = TC // 64
MID = 32
RING = 4
NU = 26
R_Q, R_FF, R_FB, R_I, R_GA, R_U, R_V, R_GB, R_MA, R_MB = range(10)
RMS_EPS = 1e-6
LN_EPS = 1e-5

ENGS = ("pe", "act", "dve", "pool", "sp")
EPOCH = 8192
NEPOCH = 8
NDMASEM = 12


class Op:
    __slots__ = ("eng", "fn", "reads", "writes", "dma", "deps", "signal",
                 "sem", "val", "qidx", "waits", "gid", "prev_slot", "phase", "sig")

    def __init__(self, eng, fn, reads, writes, dma):
        self.eng = eng
        self.fn = fn
        self.reads = reads
        self.writes = writes
        self.dma = dma
        self.deps = []
        self.signal = False
        self.sem = None
        self.val = None
        self.waits = []
        self.prev_slot = None
        self.sig = None


class Sched:
    def __init__(self, nc):
        self.nc = nc
        self.ops = []
        self.q = {e: [] for e in ENGS}
        self.last_writer = {}
        self.readers = defaultdict(dict)
        self.phase = ""

    def add(self, eng, fn, reads=(), writes=(), dma=False):
        op = Op(eng, fn, tuple(reads), tuple(writes), dma)
        op.gid = len(self.ops)
        op.phase = self.phase
        op.qidx = len(self.q[eng])
        deps = {}
        for r in op.reads:
            w = self.last_writer.get(r)
            if w is not None:
                deps[w.gid] = w
            if r.startswith("ps:"):
                for key, o in self.readers[r].items():
                    if key != op.eng:
                        deps[o.gid] = o
        for t in op.writes:
            w = self.last_writer.get(t)
            if w is not None:
                deps[w.gid] = w
            for o in self.readers[t].values():
                deps[o.gid] = o
        deps.pop(op.gid, None)
        op.deps = list(deps.values())
        for r in op.reads:
            key = ("d", op.gid) if dma else op.eng
            self.readers[r][key] = op
        for t in op.writes:
            self.last_writer[t] = op
            self.readers[t] = {}
        self.ops.append(op)
        self.q[eng].append(op)
        return op

    def pe(self, fn, reads=(), writes=()):
        return self.add("pe", fn, reads, writes)

    def act(self, fn, reads=(), writes=()):
        return self.add("act", fn, reads, writes)

    def dve(self, fn, reads=(), writes=()):
        return self.add("dve", fn, reads, writes)

    def pool(self, fn, reads=(), writes=()):
        return self.add("pool", fn, reads, writes)

    def dma(self, eng, fn, reads=(), writes=()):
        return self.add(eng, fn, reads, writes, dma=True)

    def _skip(self, d, op):
        if d.dma or op.dma or d.eng != op.eng:
            return False
        if d.eng == "pe":
            return True
        return op.qidx - d.qidx > 2

    def finalize(self, stack, out_dma_ops):
        nc = self.nc
        for op in self.ops:
            for d in op.deps:
                if d.dma or self._skip(d, op):
                    continue
                d.signal = True
        esem = {e: [stack.enter_context(nc.semaphore(f"s_{e}{i}")) for i in range(NEPOCH)] for e in ENGS}
        dsem = {e: [stack.enter_context(nc.semaphore(f"d_{e}{i}")) for i in range(NDMASEM)]
                for e in ("sp", "pool")}
        for e in ENGS:
            nsig = 0
            ndma = 0
            for op in self.q[e]:
                if op.dma:
                    op.sem = dsem[e][ndma % NDMASEM]
                    op.val = 16 * (ndma // NDMASEM + 1)
                    op.prev_slot = (op.sem, op.val - 16) if ndma >= NDMASEM else None
                    ndma += 1
                elif op.signal:
                    ep = nsig // EPOCH
                    assert ep < NEPOCH, f"too many signals on {e}"
                    op.sem = esem[e][ep]
                    op.val = nsig % EPOCH + 1
                    nsig += 1
        for e in ENGS:
            seen = {}
            for op in self.q[e]:
                want = {}
                for d in op.deps:
                    if not d.dma and (not d.signal or self._skip(d, op)):
                        continue
                    k = id(d.sem)
                    if want.get(k, (None, 0))[1] < d.val:
                        want[k] = (d.sem, d.val)
                if op.prev_slot is not None:
                    s, v = op.prev_slot
                    k = id(s)
                    if want.get(k, (None, 0))[1] < v:
                        want[k] = (s, v)
                for k, (s, v) in want.items():
                    if seen.get(k, 0) >= v:
                        continue
                    seen[k] = v
                    op.waits.append((s, v))
        block = stack.enter_context(nc.Block())
        handles = {"pe": block.tensor, "act": block.scalar, "dve": block.vector,
                   "pool": block.gpsimd, "sp": block.sync}

        def make(e):
            def body(eng):
                for op in self.q[e]:
                    for s, v in op.waits:
                        eng.wait_ge(s, v)
                    ins = op.fn(eng)
                    if op.dma:
                        ins.then_inc(op.sem, 16)
                    elif op.signal:
                        ins.then_inc(op.sem, 1)
                if e == "sp":
                    for op in out_dma_ops:
                        eng.wait_ge(op.sem, op.val)
            return body

        for e in ENGS:
            handles[e](make(e))


def v3(ap, b):
    return ap.rearrange("p (a b) -> p a b", b=b)


def build_nc(SEQ, DEPTH):
    NCH = SEQ // TC
    L = DEPTH
    nc = bass.Bass("TRN2", target_bir_lowering=False)

    def din(name, shape):
        return nc.dram_tensor(name, shape, F32, kind="ExternalInput").ap()

    x_d = din("x", [SEQ, D])
    normw_d = din("norm_w", [L, D])
    win_d = din("w_in", [L, D, 10 * D])
    lbs_d = din("lower_bounds", [L, 2, D])
    gnw_d = din("gnorm_w", [L, 128])
    lnw_d = din("ln_w", [L, D])
    lnb_d = din("ln_b", [L, D])
    ws_d = din("w_s", [L, 8, 128, 128])
    bs_d = din("b_s", [L, 8, 128])
    wa_d = din("w_proj_a", [L, D, D])
    wb_d = din("w_proj_b", [L, D, D])
    wo_d = din("w_out", [L, D, D])
    fnw_d = din("final_norm_w", [D])
    out_d = nc.dram_tensor("out", [SEQ, D], F32, kind="ExternalOutput").ap()
    wbd = nc.dram_tensor("wbf", [L, NU, 128, KC * 512], BF16, kind="Internal").ap()
    obd = nc.dram_tensor("obwd", [128, H, SEQ], F32, kind="Internal").ap()
    qsd = nc.dram_tensor("qsilu", [128, H, SEQ], BF16, kind="Internal").ap()
    vtd = nc.dram_tensor("vtok", [SEQ // TC, 128, NT * D], BF16, kind="Internal").ap()

    with ExitStack() as st:
        def sb(name, shape, dt):
            return st.enter_context(nc.sbuf_tensor(name, shape, dt))

        def pst(name, shape, dt=F32):
            return st.enter_context(nc.psum_tensor(name, shape, dt))

        S = Sched(nc)
        out_ops = []

        ident = sb("ident", [128, 128], F32)
        identb = sb("identb", [128, 128], BF16)
        ones = sb("ones", [128, 128], F32)
        onesb = sb("onesb", [128, 128], BF16)
        tri = [sb("tri_f", [128, 128], F32), sb("tri_b", [128, 128], F32)]
        rmask = sb("rmask", [128, TC], F32)
        A = sb("bigA", [128, H, TC], F32)
        B = sb("bigB", [128, H, TC], F32)
        bb = [sb(f"b{i}", [128, H, TC], BF16) for i in range(1, 7)]
        b1, b2, b3, b4, b5, b6 = bb
        xtb = [sb(f"xt{i}", [128, NT, D], F32) for i in range(2)]
        hTb = [sb(f"hT{i}", [128, KC, TC], BF16) for i in range(2)]
        ring = sb("ring", [128, RING, KC, 512], BF16)
        lnw = sb("lnw", [128, D], F32)
        lnb = sb("lnb", [128, D], F32)
        fnw = sb("fnw", [128, D], F32)
        bsrow = sb("bsrow", [1, D], F32)
        wsT = sb("wsT", [128, 8, 128], BF16)
        NTMP = 5
        tmp = [sb(f"tmp{i}", [128, TC], F32) for i in range(NTMP)]
        junk = sb("junk", [128, D], BF16)
        Sst = sb("Sst", [128, H, 128], F32)
        Stl = sb("Stl", [128, H, 128], BF16)
        am = sb("am", [128, H, 128], BF16)
        kTs = sb("kTs", [128, H, 128], BF16)
        args = sb("args", [128, 3, H, NSC], F32)
        sc = sb("sc", [128, 3, H, NSC], F32)
        ssb = sb("ssb", [128, 2, NT], F32)
        lrb = sb("lrb", [128, 2, NT], F32)
        rsb = sb("rsb", [128, 2, NT], F32)
        irsb = sb("irsb", [128, 2, NT], F32)
        ssf = sb("ssf", [128, NT], F32)
        rsf = sb("rsf", [128, NT], F32)
        st6 = sb("st6", [128, NT, 2, 6], F32)
        mv = sb("mv", [128, NT, 2], F32)
        rs2 = sb("rs2", [128, NT], F32)
        nw = sb("nw", [128, L * KC], F32)
        gw = sb("gw", [128, L], F32)
        praw = tmp[0][:, 0:128]
        lbT = tmp[1][:, 0:L * 16].rearrange("p (l x) -> p l x", x=16)
        lbe = tmp[2][:, 0:L * 16].rearrange("p (l x) -> p l x", x=16)
        lbm = sb("lbm", [128, 16], F32)
        lbv = sb("lbv", [128, L, 16], F32)
        oml = sb("oml", [128, L, 16], F32)
        noml = sb("noml", [128, L, 16], F32)
        lnc_t = sb("lnc_t", [128, 1], F32)

        pb = [pst(f"pb{i}", [128, 512], F32) if i != 2 else pst("pb2", [128, 1024], BF16) for i in range(8)]
        bank_first = [True] * 8

        def PS(i):
            return f"ps:{i}"

        tctr = [0]

        def newtmp():
            i = tctr[0] % NTMP
            tctr[0] += 1
            return tmp[i], f"tmp{i}"

        def MM(bank, out, lhsT, rhs, R):
            stt = bank_first[bank]
            bank_first[bank] = False
            op = S.pe(lambda e, o=out, l=lhsT, r=rhs, a=stt: e.matmul(o, lhsT=l, rhs=r, start=a, stop=False,
                                                                      skip_group_check=True),
                      reads=R, writes=[PS(bank)])
            op.sig = ("mm", lhsT.shape[0], int(np.prod(rhs.shape[1:])), str(rhs.dtype))

        def TR(bank, out, in_, idn, R):
            bank_first[bank] = False
            op = S.pe(lambda e, o=out, i=in_, d=idn: e.transpose(o, i, d), reads=R, writes=[PS(bank)])
            op.sig = ("tr", in_.shape[0], int(np.prod(in_.shape[1:])), str(in_.dtype))

        def release(bank):
            bank_first[bank] = True

        def ACT(out, in_, func, R, W, scale=1.0, bias=None, accum=None):
            kw = {}
            if bias is not None:
                kw["bias"] = bias
            if accum is not None:
                kw["accum_out"] = accum
            S.act(lambda e, o=out, i=in_, f=func, s=scale, kw=kw: e.activation(out=o, in_=i, func=f, scale=s, **kw),
                  reads=R, writes=W)

        def TT(eng, out, in0, in1, op, R, W):
            S.add(eng, lambda e, o=out, a=in0, b=in1, p=op: e.tensor_tensor(out=o, in0=a, in1=b, op=p), reads=R, writes=W)

        def TS(eng, out, in0, s1, s2, op0, op1, R, W):
            if s2 is None:
                S.add(eng, lambda e, o=out, a=in0, x=s1, p=op0:
                      e.tensor_scalar(out=o, in0=a, scalar1=x, scalar2=None, op0=p), reads=R, writes=W)
            else:
                S.add(eng, lambda e, o=out, a=in0, x=s1, y=s2, p=op0, q=op1:
                      e.tensor_scalar(out=o, in0=a, scalar1=x, scalar2=y, op0=p, op1=q), reads=R, writes=W)

        def STT(out, in0, scalar, in1, op0, op1, R, W):
            S.dve(lambda e, o=out, a=in0, s=scalar, b=in1, p=op0, q=op1:
                  e.scalar_tensor_tensor(out=o, in0=a, scalar=s, in1=b, op0=p, op1=q), reads=R, writes=W)

        def CP(eng, out, in_, R, W):
            S.add(eng, lambda e, o=out, i=in_: e.tensor_copy(out=o, in_=i), reads=R, writes=W)

        def MEMSET(eng, out, val, W):
            S.add(eng, lambda e, o=out, v=val: e.memset(o, v), writes=W)

        def DMA(q, out, in_, R, W):
            return S.dma(q, lambda e, o=out, i=in_: e.dma_start(out=o, in_=i), reads=R, writes=W)

        def hn(base, hs=range(H)):
            return [f"{base}:{h}" for h in hs]

        def wsrc(l, u):
            if u < 20:
                src = win_d[l, :, u * 512:(u + 1) * 512]
            elif u < 22:
                src = wa_d[l, :, (u - 20) * 512:(u - 19) * 512]
            elif u < 24:
                src = wb_d[l, :, (u - 22) * 512:(u - 21) * 512]
            else:
                src = wo_d[l, :, (u - 24) * 512:(u - 23) * 512]
            return src.rearrange("(kc p) c -> p kc c", p=128)

        def cast_unit(l, u):
            DMA("pool", wbd[l, u].rearrange("p (kc c) -> p kc c", c=512), wsrc(l, u), [], [f"wbd:{l}:{u}"])

        U_WA, U_WB, U_WO = 20, 22, 24

        def pass_units(l, pas):
            if pas == 1:
                ul = [2 * r + k for r in (R_FB, R_Q, R_I) for k in range(2)]
            else:
                ul = [2 * r + k for r in (R_FF, R_GA, R_U, R_GB, R_V, R_MA) for k in range(2)]
                ul += [U_WA, U_WA + 1, 2 * R_MB, 2 * R_MB + 1, U_WB, U_WB + 1, U_WO, U_WO + 1]
            return [(l, u) for u in ul]

        plan = []
        for l in range(L):
            for pas in (1, 2):
                for _ in range(NCH):
                    plan += pass_units(l, pas)
        wstate = {"ptr": 0, "issued": 0}

        def w_issue_upto(n):
            while wstate["issued"] < min(len(plan), n):
                i = wstate["issued"]
                l, u = plan[i]
                slot = i % RING
                DMA("sp", ring[:, slot].rearrange("p kc c -> p (kc c)"), wbd[l, u], [f"wbd:{l}:{u}"], [f"w:{slot}"])
                wstate["issued"] += 1

        def wunit(l, u):
            p = wstate["ptr"]
            assert plan[p] == (l, u), (plan[p], l, u)
            w_issue_upto(p + RING)
            wstate["ptr"] += 1
            slot = p % RING
            return slot, [f"w:{slot}"]

        def cast_order(l):
            first = [u for (_, u) in pass_units(l, 1)]
            rest = [u for (_, u) in pass_units(l, 2) if u not in first]
            rest += [u for u in range(NU) if u not in first and u not in rest and not (2 * R_FB <= u < 2 * R_FB + 2)]
            return first + rest

        assert sorted(cast_order(0)) == list(range(NU)), cast_order(0)
        cast_q = []

        def emit_casts(n):
            for _ in range(n):
                if cast_q:
                    l, u = cast_q.pop(0)
                    cast_unit(l, u)

        MEMSET("pool", ident[:], 1.0, ["ident"])
        S.pool(lambda e: e.affine_select(out=ident[:], in_=ident[:], pattern=[[-1, 128]], compare_op=ALU.is_equal,
                                         fill=0.0, base=0, channel_multiplier=1), reads=["ident"], writes=["ident"])
        CP("pool", identb[:], ident[:], ["ident"], ["identb"])
        MEMSET("pool", ones[:], 1.0, ["ones"])
        MEMSET("pool", onesb[:], 1.0, ["onesb"])
        MEMSET("pool", tri[0][:], 1.0, ["tri0"])
        S.pool(lambda e: e.affine_select(out=tri[0][:], in_=tri[0][:], pattern=[[1, 128]], compare_op=ALU.is_ge,
                                         fill=0.0, base=0, channel_multiplier=-1), reads=["tri0"], writes=["tri0"])
        MEMSET("pool", tri[0][0:64, 64:128], 0.0, ["tri0"])
        MEMSET("pool", tri[1][:], 1.0, ["tri1"])
        S.pool(lambda e: e.affine_select(out=tri[1][:], in_=tri[1][:], pattern=[[-1, 128]], compare_op=ALU.is_ge,
                                         fill=0.0, base=0, channel_multiplier=1), reads=["tri1"], writes=["tri1"])
        MEMSET("pool", tri[1][64:128, 0:64], 0.0, ["tri1"])
        MEMSET("pool", rmask[:], 1.0, ["rmask"])
        MEMSET("pool", v3(rmask[:], 64)[:, :, 0:1], 0.0, ["rmask"])
        LNC = math.log(1.0 / math.sqrt(128.0))
        MEMSET("pool", lnc_t[:], LNC, ["lnc"])

        for u in cast_order(0):
            cast_unit(0, u)

        def load_T(dst, src2d, nrows):
            DMA("sp", praw[0:nrows, :], src2d, [], ["tmp0"])
            TR(7, pb[7][:, 0:nrows], praw[0:nrows, :], ident[0:nrows, 0:nrows], ["tmp0", "ident"])
            CP("dve", dst, pb[7][:, 0:nrows], [PS(7)], ["params"])
            release(7)

        load_T(nw[:], normw_d.rearrange("l (kc p) -> (l kc) p", p=128), L * KC)
        load_T(gw[:], gnw_d, L)
        load_T(lbT.rearrange("p l x -> p (l x)"), lbs_d.rearrange("l d (h p) -> (l d h) p", p=128), L * 16)
        DMA("sp", fnw[:], fnw_d.partition_broadcast(128), [], ["fnw"])
        CP("dve", lbm[:], lbT[:, 0, :], ["params", "tmp1"], ["lbm"])
        for l in range(1, L):
            TT("dve", lbm[:], lbm[:], lbT[:, l, :], ALU.max, ["params", "tmp1", "lbm"], ["lbm"])
        for l in range(L):
            TT("dve", lbe[:, l, :], lbT[:, l, :], lbm[:], ALU.subtract, ["params", "tmp1", "lbm"], ["tmp2"])
        ACT(lbe.rearrange("p l x -> p (l x)"), lbe.rearrange("p l x -> p (l x)"), AF.Exp, ["tmp2"], ["tmp2"])
        CP("dve", lbm[:], lbe[:, 0, :], ["tmp2"], ["lbm"])
        for l in range(1, L):
            TT("dve", lbm[:], lbm[:], lbe[:, l, :], ALU.add, ["tmp2", "lbm"], ["lbm"])
        S.dve(lambda e: e.reciprocal(out=lbm[:], in_=lbm[:]), reads=["lbm"], writes=["lbm"])
        for l in range(L):
            TT("dve", lbe[:, l, :], lbe[:, l, :], lbm[:], ALU.mult, ["tmp2", "lbm"], ["tmp2"])
        MEMSET("dve", lbv[:, 0, :], 0.0, ["lbv"])
        for l in range(1, L):
            TT("dve", lbv[:, l, :], lbv[:, l - 1, :], lbe[:, l, :], ALU.add, ["tmp2", "lbv"], ["lbv"])
        TS("dve", oml[:].rearrange("p l x -> p (l x)"), lbv[:].rearrange("p l x -> p (l x)"), -1.0, 1.0, ALU.mult, ALU.add,
           ["lbv"], ["lbc"])
        TS("dve", noml[:].rearrange("p l x -> p (l x)"), lbv[:].rearrange("p l x -> p (l x)"), 1.0, -1.0, ALU.mult, ALU.add,
           ["lbv"], ["lbc"])

        steps = []
        for l in range(L):
            for j in range(NCH - 1, -1, -1):
                steps.append((l, 1, j))
            for j in range(NCH):
                steps.append((l, 2, j))
        sbuf_of = []
        share = []
        for i, (l, pas, j) in enumerate(steps):
            if i == 0:
                sbuf_of.append(0); share.append("none")
            else:
                pl, pp, pj = steps[i - 1]
                if pl == l and pp == 1 and pas == 2 and pj == j:
                    sbuf_of.append(sbuf_of[-1]); share.append("same")
                elif pp == 2 and pas == 1 and pl + 1 == l and pj == j:
                    sbuf_of.append(sbuf_of[-1]); share.append("carry")
                else:
                    sbuf_of.append(1 - sbuf_of[-1]); share.append("none")

        def hTn(bf):
            return [f"hT{bf}:{kc}" for kc in range(KC)]

        def load_x(l, j, bf):
            src = x_d if l == 0 else out_d
            DMA("sp", xtb[bf][:], src[j * TC:(j + 1) * TC, :].rearrange("(t p) d -> p t d", p=128),
                [f"xd:{j}"] if l > 0 else [], [f"xt{bf}"])

        def rms_stats(bf, do_scale=True):
            xt = xtb[bf]
            xn = f"xt{bf}"
            for tt in range(NT):
                ACT(junk[:], xt[:, tt, :], AF.Square, [xn], ["junk", f"ss{bf}"], accum=ssb[:, bf, tt:tt + 1])
            ACT(lrb[:, bf, :], ssb[:, bf, :], AF.Ln, [f"ss{bf}"], [f"lr{bf}"], scale=1.0 / D, bias=RMS_EPS)
            ACT(rsb[:, bf, :], lrb[:, bf, :], AF.Exp, [f"lr{bf}"], [f"rs{bf}"], scale=-0.5)
            ACT(irsb[:, bf, :], lrb[:, bf, :], AF.Exp, [f"lr{bf}"], [f"irs{bf}"], scale=0.5)
            if do_scale:
                rms_scale(bf)

        def rms_scale(bf):
            xt = xtb[bf]
            xn = f"xt{bf}"
            for tt in range(NT):
                TS("dve", xt[:, tt, :], xt[:, tt, :], rsb[:, bf, tt:tt + 1], None, ALU.mult, ALU.bypass,
                   [xn, f"rs{bf}"], [xn])

        def rms_transposes(l, bf, banks):
            xt = xtb[bf]
            xn = f"xt{bf}"
            for kc in range(KC):
                bank = banks[kc % len(banks)]
                for tt in range(NT):
                    TR(bank, pb[bank][:, tt * 128:(tt + 1) * 128], xt[:, tt, kc * 128:(kc + 1) * 128], ident[:],
                       [xn, "ident"])
                ACT(hTb[bf][:, kc, :], pb[bank][:], AF.Copy, [PS(bank), "params"], [f"hT{bf}:{kc}"],
                    scale=nw[:, l * KC + kc:l * KC + kc + 1])
                release(bank)

        def prep_load(i):
            if i < len(steps) and share[i] == "none":
                l, pas, j = steps[i]
                load_x(l, j, sbuf_of[i])

        def prep_stats(i, do_scale=True):
            if i < len(steps) and share[i] != "same":
                rms_stats(sbuf_of[i], do_scale)

        def prep_tr(i, banks):
            if i < len(steps) and share[i] != "same":
                rms_transposes(steps[i][0], sbuf_of[i], banks)

        FM_BANKS = (0, 1, 3, 4)

        def proj_fm_unit(l, u, bf, evac):
            slot, wn = wunit(l, u)
            for jj in range(4):
                bank = FM_BANKS[jj]
                for kc in range(KC):
                    MM(bank, pb[bank][:], ring[:, slot, kc, jj * 128:(jj + 1) * 128], hTb[bf][:, kc, :], wn + [f"hT{bf}:{kc}"])
                evac(jj, bank)
                release(bank)

        def proj_fm_region(l, reg, bf, evac):
            for k in range(2):
                proj_fm_unit(l, 2 * reg + k, bf, lambda jj, bank, k=k: evac(4 * k + jj, bank))

        def proj_tm(l, u, rhs_src, rhs_names, evac):
            slot, wn = wunit(l, u)
            for tt in range(NT):
                bank = 3 + (tt % 2)
                for kc in range(KC):
                    MM(bank, pb[bank][:], rhs_src[:, kc, tt * 128:(tt + 1) * 128], ring[:, slot, kc, :],
                       wn + [rhs_names[kc]])
                evac(tt, bank)
                release(bank)

        Bf = B[:].rearrange("p h t -> p (h t)")
        Af = A[:].rearrange("p h t -> p (h t)")
        b1f = b1[:].rearrange("p h t -> p (h t)")
        b2f = b2[:].rearrange("p h t -> p (h t)")
        b3f = b3[:].rearrange("p h t -> p (h t)")
        b6f = b6[:].rearrange("p h t -> p (h t)")

        def gate_proj(l, dirn, bf, after_first=None):
            reg = R_FF if dirn == 0 else R_FB

            def evac(h, bank):
                t, tn = newtmp()
                ACT(t[:], pb[bank][:], AF.Sigmoid, [PS(bank)], [tn])
                col = dirn * 8 + h
                TS("dve", B[:, h, :], t[:], oml[:, l, col:col + 1], lbv[:, l, col:col + 1], ALU.mult, ALU.add,
                   [tn, "lbc", "lbv"], [f"B:{h}"])
                TS("dve", b3[:, h, :], t[:], noml[:, l, col:col + 1], oml[:, l, col:col + 1], ALU.mult, ALU.add,
                   [tn, "lbc"], [f"b3:{h}"])
            proj_fm_unit(l, 2 * reg, bf, lambda jj, bank: evac(jj, bank))
            if after_first is not None:
                after_first()
            proj_fm_unit(l, 2 * reg + 1, bf, lambda jj, bank: evac(4 + jj, bank))

        def gate_chain1(dirn):
            ACT(Bf, Bf, AF.Ln, hn("B"), hn("B"))
            for h in range(H):
                S.dve(lambda e, h=h: e.tensor_tensor_scan(out=A[:, h, :], data0=rmask[:], data1=B[:, h, :], initial=0.0,
                                                          op0=ALU.mult, op1=ALU.add),
                      reads=["rmask", f"B:{h}"], writes=[f"A:{h}"])
            A4 = A[:].rearrange("p h (c t) -> p h c t", t=64)
            B4 = B[:].rearrange("p h (c t) -> p h c t", t=64)
            A3 = Af.rearrange("p (g t) -> p g t", t=64)
            B3 = Bf.rearrange("p (g t) -> p g t", t=64)
            if dirn == 0:
                CP("pool", args[:, 0], A4[:, :, :, MID], hn("A"), ["args"])
                CP("pool", args[:, 1], A4[:, :, :, 63], hn("A"), ["args"])
                TT("pool", args[:, 2], A4[:, :, :, 63], A4[:, :, :, MID], ALU.subtract, hn("A"), ["args"])
                TT("dve", B3, A3, A3[:, :, MID:MID + 1].to_broadcast([128, H * NSC, 64]), ALU.subtract, hn("A"), hn("B"))
            else:
                TT("dve", Bf, Af, Bf, ALU.subtract, hn("A") + hn("B"), hn("B"))
                TT("pool", args[:, 0], A4[:, :, :, 63], B4[:, :, :, MID], ALU.subtract, hn("A") + hn("B"), ["args"])
                CP("pool", args[:, 1], A4[:, :, :, 63], hn("A"), ["args"])
                CP("pool", args[:, 2], B4[:, :, :, MID], hn("B"), ["args"])
                TT("dve", A3, B3, B3[:, :, MID:MID + 1].to_broadcast([128, H * NSC, 64]), ALU.subtract,
                   hn("B") + ["args"], hn("A"))

        def chain2_sc():
            ACT(sc[:].rearrange("p a h c -> p (a h c)"), args[:].rearrange("p a h c -> p (a h c)"), AF.Exp, ["args"], ["sc"])

        def chain2_head(dirn, h):
            if dirn == 0:
                Dn, Dh, sgn = f"B:{h}", B[:, h, :], 1.0
            else:
                Dn, Dh, sgn = f"A:{h}", A[:, h, :], -1.0
            ACT(b6[:, h, :], Dh, AF.Exp, [Dn, "lnc"], [f"b6:{h}"], scale=sgn, bias=lnc_t[:, 0:1])
            ACT(b2[:, h, :], Dh, AF.Exp, [Dn], [f"b2:{h}"], scale=-sgn)
            TT("dve", b2[:, h, :], b2[:, h, :], b3[:, h, :], ALU.mult, [f"b2:{h}", f"b3:{h}"], [f"b2:{h}"])
            TT("dve", b1[:, h, :], b1[:, h, :], b6[:, h, :], ALU.mult, [f"b1:{h}", f"b6:{h}"], [f"b1:{h}"])
            TT("pool", v3(b3[:, h, :], 64), v3(b2[:, h, :], 64),
               sc[:, 2, h, :].unsqueeze(2).to_broadcast([128, NSC, 64]), ALU.mult, [f"b2:{h}", "sc"], [f"b3:{h}"])

        def q_phase(l, bf):
            proj_fm_region(l, R_Q, bf, lambda h, bank: ACT(b1[:, h, :], pb[bank][:], AF.Silu, [PS(bank)], [f"b1:{h}"]))

        def ga_phase(l, bf):
            proj_fm_region(l, R_GA, bf, lambda h, bank: ACT(b5[:, h, :], pb[bank][:], AF.Silu, [PS(bank)], [f"b5:{h}"]))

        b4v = b4[:].rearrange("p h t -> p (h t)").rearrange("p (t d) -> p t d", d=D)
        b3v = b3[:].rearrange("p h t -> p (h t)").rearrange("p (t d) -> p t d", d=D)

        def i_phase(l, bf, dirn):
            chain2_sc()
            for u2 in range(2):
                def evac(tt, bank, u2=u2):
                    ACT(b4v[:, tt, u2 * 512:(u2 + 1) * 512], pb[bank][:], AF.Copy, [PS(bank)], hn("b4"))
                    chain2_head(dirn, 4 * u2 + tt)
                proj_tm(l, 2 * R_I + u2, hTb[bf], hTn(bf), evac)

        def gla(dirn, casts=0):
            tiles = range(NT) if dirn == 0 else range(NT - 1, -1, -1)
            GR = (range(0, 4), range(4, 8))
            for tt in tiles:
                cols = slice(tt * 128, (tt + 1) * 128)
                for g in range(2):
                    hs = GR[g]
                    for h in hs:
                        MM(g, pb[g][:, (h % 4) * 128:(h % 4 + 1) * 128], b2[:, h, cols], b1[:, h, cols],
                           [f"b2:{h}", f"b1:{h}"])
                    TT("dve", am[:, g * 4:(g + 1) * 4, :], v3(pb[g][:], 128),
                       tri[dirn][:].unsqueeze(1).to_broadcast([128, 4, 128]), ALU.mult,
                       [PS(g), f"tri{dirn}"], [f"am{g}"])
                    release(g)
                    kb = 2 if g == 0 else 7
                    kview = pb[2][:, 0:512] if g == 0 else pb[7][:].bitcast(BF16)[:, 0:512]
                    for h in hs:
                        TR(kb, kview[:, (h % 4) * 128:(h % 4 + 1) * 128], b3[:, h, cols], identb[:], [f"b3:{h}", "identb"])
                    ACT(kTs[:, g * 4:(g + 1) * 4, :], v3(kview, 128), AF.Copy, [PS(kb)], [f"kTs{g}"])
                    release(kb)
                    for h in hs:
                        MM(3 + g, pb[3 + g][:, (h % 4) * 128:(h % 4 + 1) * 128], b4v[:, tt, h * 128:(h + 1) * 128], am[:, h, :],
                           hn("b4") + [f"am{g}"])
                subs = (2 * tt, 2 * tt + 1) if dirn == 0 else (2 * tt + 1, 2 * tt)
                for c in subs:
                    half = c % 2
                    rows = slice(half * 64, half * 64 + 64)
                    ccols = slice(c * 64, c * 64 + 64)
                    for g in range(2):
                        hs = GR[g]
                        e1 = "dve" if g == 0 else "pool"
                        hsl = slice(g * 4, (g + 1) * 4)
                        TT(e1, Stl[:, hsl, :], Sst[:, hsl, :], sc[:, 0, hsl, c:c + 1].to_broadcast([128, 4, 128]), ALU.mult,
                           [f"Sst{g}", "sc"], [f"Stl{g}"])
                        for h in hs:
                            o0 = (h % 4) * 128 + half * 64
                            MM(3 + g, pb[3 + g][:, o0:o0 + 64], Stl[:, h, :], b1[:, h, ccols], [f"Stl{g}", f"b1:{h}"])
                        for h in hs:
                            MM(5 + g, pb[5 + g][:, (h % 4) * 128:(h % 4 + 1) * 128], kTs[rows, h, :],
                               b4v[rows, tt, h * 128:(h + 1) * 128], [f"kTs{g}"] + hn("b4"))
                        TT(e1, Sst[:, hsl, :], Sst[:, hsl, :], sc[:, 1, hsl, c:c + 1].to_broadcast([128, 4, 128]), ALU.mult,
                           [f"Sst{g}", "sc"], [f"Sst{g}"])
                        TT("dve", Sst[:, hsl, :], Sst[:, hsl, :], v3(pb[5 + g][:], 128), ALU.add, [f"Sst{g}", PS(5 + g)],
                           [f"Sst{g}"])
                        release(5 + g)
                for g in range(2):
                    ACT(B[:, g * 4:(g + 1) * 4, cols], v3(pb[3 + g][:], 128), AF.Copy, [PS(3 + g)],
                        [f"B:{h}" for h in GR[g]])
                    release(3 + g)
                emit_casts(casts)

        def layer_setup(l):
            DMA("sp", lnw[:], lnw_d[l].partition_broadcast(128), [], ["lnw"])
            DMA("sp", lnb[:], lnb_d[l].partition_broadcast(128), [], ["lnb"])
            DMA("sp", bsrow[:], bs_d[l].rearrange("g t -> (g t)").unsqueeze(0), [], ["bsrow"])
            wstage = A[:, 0:2, :].rearrange("p h t -> p (h t)").rearrange("p (g s) -> p g s", s=128)
            DMA("sp", wstage, ws_d[l].rearrange("g t s -> t g s"), [], hn("A"))
            for g in range(8):
                bk = 5 + g // 4
                TR(bk, pb[bk][:, (g % 4) * 128:(g % 4 + 1) * 128], wstage[:, g, :], ident[:], hn("A") + ["ident"])
            for bk in range(2):
                CP("dve", wsT[:, bk * 4:(bk + 1) * 4, :], v3(pb[5 + bk][:], 128), [PS(5 + bk)], ["wsT"])
                release(5 + bk)

        def chunk_pass1(i):
            l, _, j = steps[i]
            bf = sbuf_of[i]
            nxt = i + 1 if (i + 1 < len(steps) and share[i + 1] == "none") else None
            S.phase = "p1.gate"
            if nxt is not None:
                prep_load(nxt)
            gate_proj(l, 1, bf)
            gate_chain1(1)
            S.phase = "p1.q"
            q_phase(l, bf)
            DMA("sp", qsd[:, :, j * TC:(j + 1) * TC], b1[:], hn("b1"), [f"qs:{j}"])
            S.phase = "p1.st"
            if nxt is not None:
                prep_stats(nxt)
            S.phase = "p1.i"
            i_phase(l, bf, 1)
            DMA("sp", vtd[j], b4[:].rearrange("p h t -> p (h t)"), hn("b4"), [f"vt:{j}"])
            S.phase = "p1.prep"
            if nxt is not None:
                prep_tr(nxt, (5, 6))
            S.phase = "p1.gla"
            gla(1)
            DMA("sp", obd[:, :, j * TC:(j + 1) * TC], B[:], hn("B"), [f"ob:{j}"])

        def chunk_pass2(i, last):
            l, _, j = steps[i]
            bf = sbuf_of[i]
            xt = xtb[bf]
            xn = f"xt{bf}"
            ncast = 1 if l + 1 < L else 0
            nxt = i + 1 if (i + 1 < len(steps) and share[i + 1] == "none") else None
            S.phase = "p2.gate"
            DMA("sp", b1[:], qsd[:, :, j * TC:(j + 1) * TC], [f"qs:{j}"], hn("b1"))
            DMA("sp", b4[:].rearrange("p h t -> p (h t)"), vtd[j], [f"vt:{j}"], hn("b4"))
            if nxt is not None:
                prep_load(nxt)
            gate_proj(l, 0, bf)
            gate_chain1(0)
            emit_casts(ncast)
            emit_casts(ncast)
            S.phase = "p2.ga"
            ga_phase(l, bf)
            emit_casts(ncast)
            S.phase = "p2.i"
            chain2_sc()
            for h in range(H):
                chain2_head(0, h)
            emit_casts(ncast)
            S.phase = "p2.u"
            proj_fm_region(l, R_U, bf,
                           lambda g, bank: ACT(b6[:, g, :], pb[bank][:], AF.Gelu_apprx_tanh, [PS(bank)], [f"b6:{g}"]))
            DMA("sp", A[:], obd[:, :, j * TC:(j + 1) * TC], [f"ob:{j}"], hn("A"))
            S.phase = "p2.gla"
            gla(0, ncast)
            S.phase = "p2.hnorm"
            TT("dve", Bf, Bf, Af, ALU.add, hn("A") + hn("B"), hn("B"))
            ACT(b3f, Bf, AF.Square, hn("B"), hn("b3"))
            S.phase = "p2.gb"
            proj_fm_region(l, R_GB, bf, lambda g, bank: ACT(b2[:, g, :], pb[bank][:], AF.Silu, [PS(bank)], [f"b2:{g}"]))
            S.phase = "p2.hnorm"
            for h in range(H):
                bank = 5 + h % 3
                MM(bank, pb[bank][:], onesb[:], b3[:, h, :], ["onesb", f"b3:{h}"])
                r, rn = newtmp()
                ACT(r[:], pb[bank][:], AF.Ln, [PS(bank)], [rn], scale=1.0 / 128, bias=RMS_EPS)
                release(bank)
                ACT(r[:], r[:], AF.Exp, [rn], [rn], scale=-0.5)
                TT("dve", r[:], r[:], B[:, h, :], ALU.mult, [rn, f"B:{h}"], [rn])
                STT(b1[:, h, :], r[:], gw[:, l:l + 1], b5[:, h, :], ALU.mult, ALU.mult, [rn, "params", f"b5:{h}"],
                    [f"b1:{h}"])
            emit_casts(ncast)
            S.phase = "p2.v"
            Av = Af.rearrange("p (t d) -> p t d", d=D)
            for u2 in range(2):
                def evac(tt, bank, u2=u2):
                    ACT(Av[:, tt, u2 * 512:(u2 + 1) * 512], pb[bank][:], AF.Gelu_apprx_tanh, [PS(bank)], hn("A"))
                proj_tm(l, 2 * R_V + u2, hTb[bf], hTn(bf), evac)
            emit_casts(ncast)
            S.phase = "p2.lns"
            for tt in range(NT):
                for k in range(2):
                    S.dve(lambda e, tt=tt, k=k: e.bn_stats(out=st6[:, tt, k, :], in_=Av[:, tt, k * 512:(k + 1) * 512]),
                          reads=hn("A"), writes=["st6"])
                S.dve(lambda e, tt=tt: e.bn_aggr(out=mv[:, tt, :], in_=st6[:, tt].rearrange("p a b -> p (a b)")),
                      reads=["st6"], writes=["mv"])
            S.phase = "p2.gbm"
            TT("dve", b2f, b2f, b6f, ALU.mult, hn("b2") + hn("b6"), hn("b2"))
            S.phase = "p2.ln"
            ACT(rs2[:], mv[:, :, 1], AF.Ln, ["mv"], ["rs2"], bias=LN_EPS)
            ACT(rs2[:], rs2[:], AF.Exp, ["rs2"], ["rs2"], scale=-0.5)
            for tt in range(NT):
                TS("dve", Av[:, tt, :], Av[:, tt, :], mv[:, tt, 0:1], rs2[:, tt:tt + 1], ALU.subtract, ALU.mult,
                   hn("A") + ["mv", "rs2"], [f"Av:{tt}"])
            for tt in range(NT):
                TT("pool", Av[:, tt, :], Av[:, tt, :], lnw[:], ALU.mult, [f"Av:{tt}", "lnw"], [f"Av:{tt}"])
            for tt in range(NT):
                TT("dve", b3v[:, tt, :], Av[:, tt, :], lnb[:], ALU.add, [f"Av:{tt}", "lnb"], hn("b3") + hn("A"))
            emit_casts(ncast)
            S.phase = "p2.pa"
            proj_fm_region(l, R_MA, bf, lambda e_, bank: ACT(b6[:, e_, :], pb[bank][:], AF.Sigmoid, [PS(bank)], [f"b6:{e_}"]))
            S.phase = "p2.st"
            if nxt is not None:
                prep_stats(nxt, do_scale=False)
            S.phase = "p2.pa"
            for k in range(2):
                slot, wn = wunit(l, U_WA + k)
                for jj in range(4):
                    e_ = 4 * k + jj
                    bank = FM_BANKS[jj]
                    for kc in range(KC):
                        MM(bank, pb[bank][:], ring[:, slot, kc, jj * 128:(jj + 1) * 128], b1[:, kc, :], wn + [f"b1:{kc}"])
                    TT("dve", B[:, e_, :], pb[bank][:], b6[:, e_, :], ALU.mult, [PS(bank), f"b6:{e_}"], [f"B:{e_}"])
                    release(bank)
            emit_casts(ncast)
            S.phase = "p2.prep"
            if nxt is not None:
                rms_scale(sbuf_of[nxt])
                prep_tr(nxt, (5, 3))
            S.phase = "p2.mix"
            for g in range(8):
                mb = 7 if g % 2 == 0 else 6
                for tt in range(NT):
                    MM(mb, pb[mb][:, tt * 128:(tt + 1) * 128], b3v[:, tt, g * 128:(g + 1) * 128], wsT[:, g, :],
                       hn("b3") + ["wsT"])
                MM(mb, v3(pb[mb][:], 128), ones[0:1, :], bsrow[0:1, g * 128:(g + 1) * 128].unsqueeze(1).to_broadcast([1, NT, 128]),
                   ["ones", "bsrow"])
                TT("dve", b4[:, g, :], pb[mb][:], b2[:, g, :], ALU.mult, [PS(mb), f"b2:{g}"], [f"b4:{g}"])
                release(mb)
            emit_casts(ncast)
            S.phase = "p2.pb"
            proj_fm_region(l, R_MB, bf, lambda e_, bank: ACT(b6[:, e_, :], pb[bank][:], AF.Sigmoid, [PS(bank)], [f"b6:{e_}"]))
            for k in range(2):
                slot, wn = wunit(l, U_WB + k)
                for jj in range(4):
                    e_ = 4 * k + jj
                    bank = FM_BANKS[jj]
                    for kc in range(KC):
                        MM(bank, pb[bank][:], ring[:, slot, kc, jj * 128:(jj + 1) * 128], b4[:, kc, :], wn + [f"b4:{kc}"])
                    sg, sgn_ = newtmp()
                    TT("dve", sg[:], pb[bank][:], b6[:, e_, :], ALU.mult, [PS(bank), f"b6:{e_}"], [sgn_])
                    release(bank)
                    TT("dve", b5[:, e_, :], sg[:], B[:, e_, :], ALU.add, [sgn_, f"B:{e_}"], [f"b5:{e_}"])
            emit_casts(ncast)
            S.phase = "p2.out"
            b5n = [f"b5:{kc}" for kc in range(KC)]
            for u2 in range(2):
                def evac(tt, bank, u2=u2):
                    STT(xt[:, tt, u2 * 512:(u2 + 1) * 512], xt[:, tt, u2 * 512:(u2 + 1) * 512], irsb[:, bf, tt:tt + 1],
                        pb[bank][:], ALU.mult, ALU.add, [PS(bank), xn, f"irs{bf}"], [xn])
                proj_tm(l, U_WO + u2, b5, b5n, evac)
            if last:
                for tt in range(NT):
                    ACT(junk[:], xt[:, tt, :], AF.Square, [xn], ["junk", "ssf"], accum=ssf[:, tt:tt + 1])
                ACT(rsf[:], ssf[:], AF.Ln, ["ssf"], ["rsf"], scale=1.0 / D, bias=RMS_EPS)
                ACT(rsf[:], rsf[:], AF.Exp, ["rsf"], ["rsf"], scale=-0.5)
                for tt in range(NT):
                    STT(xt[:, tt, :], xt[:, tt, :], rsf[:, tt:tt + 1], fnw[:], ALU.mult, ALU.mult, [xn, "rsf", "fnw"], [xn])
            op = DMA("sp", out_d[j * TC:(j + 1) * TC, :].rearrange("(t p) d -> p t d", p=128), xt[:], [xn], [f"xd:{j}"])
            if last:
                out_ops.append(op)
            emit_casts(ncast)

        for i, (l, pas, j) in enumerate(steps):
            if pas == 1 and j == NCH - 1:
                layer_setup(l)
                MEMSET("pool", Sst[:], 0.0, ["Sst0", "Sst1"])
                if l + 1 < L:
                    cast_q.extend((l + 1, g) for g in cast_order(l + 1))
                S.phase = "prep0"
                prep_load(i)
                prep_stats(i)
                prep_tr(i, (5, 6))
            if pas == 2 and j == 0:
                MEMSET("pool", Sst[:], 0.0, ["Sst0", "Sst1"])
            if pas == 1:
                chunk_pass1(i)
            else:
                chunk_pass2(i, l == L - 1)
            if pas == 2 and j == NCH - 1:
                emit_casts(len(cast_q))
        assert wstate["ptr"] == len(plan)
        assert not cast_q
        S.finalize(st, out_ops)
        global LAST_SCHED
        LAST_SCHED = S
    return nc


LAST_SCHED = None
_NC_CACHE = {}


def _get_nc(seq, depth):
    key = (seq, depth)
    if key not in _NC_CACHE:
        _NC_CACHE[key] = build_nc(seq, depth)
    return _NC_CACHE[key]


def kernel(x, norm_w, w_in, lower_bounds, gnorm_w, ln_w, ln_b, w_s, b_s, w_proj_a, w_proj_b, w_out, final_norm_w):
    x = np.asarray(x, dtype=np.float32)
    Bn, SEQ, _ = x.shape
    DEPTH = norm_w.shape[0]
    nc = _get_nc(SEQ, DEPTH)
    shared = {
        "norm_w": norm_w, "w_in": w_in, "lower_bounds": lower_bounds, "gnorm_w": gnorm_w, "ln_w": ln_w, "ln_b": ln_b,
        "w_s": w_s, "b_s": b_s, "w_proj_a": w_proj_a, "w_proj_b": w_proj_b, "w_out": w_out, "final_norm_w": final_norm_w,
    }
    shared = {k: np.ascontiguousarray(np.asarray(v, dtype=np.float32)) for k, v in shared.items()}
    in_maps = [dict(shared, x=np.ascontiguousarray(x[b])) for b in range(Bn)]
    res = run_bass_kernel_spmd(nc, in_maps, core_ids=list(range(Bn)))
    return np.stack([np.asarray(r["out"]) for r in res.results], axis=0).astype(np.float32)
```
